# Optimizing a Trainium2 kernel written in Bass

```python
import jax, jax.numpy as jnp
from jax import lax
import numpy as np

D_MODEL = 4096
BATCH = 4
SEQ = 4096
DEPTH = 2
DEC_BATCH = 16
DEC_SEQ = 64
PAST_LEN = 4096

CHUNK = 64
MIX_WIDTH = D_MODEL
M_WIDTH = MIX_WIDTH // 2
H_WIDTH = MIX_WIDTH - M_WIDTH
M_HEADS = 4
M_DV = M_WIDTH // M_HEADS
M_DK = M_DV // 2
M_QK = M_HEADS * M_DK
H_DK = 128
H_HEADS = H_WIDTH // H_DK
H_DV = H_WIDTH // H_HEADS
D_FF = 4 * D_MODEL
HG_BLOCK = 16
EPS = 1e-6
NEG_BIG = -1e30
LB_FLOOR = 1e-30
IN_SIZES = (M_QK, M_QK, M_WIDTH, M_WIDTH, M_HEADS, M_HEADS, H_WIDTH, H_WIDTH, H_WIDTH, H_WIDTH)
IN_COLS = sum(IN_SIZES)

kernel_name = "hymba_mlstm_hgrn2_streaming_step"


def _rmsnorm(x, g):
    xf = x.astype(jnp.float32)
    y = xf * lax.rsqrt(jnp.mean(xf * xf, axis=-1, keepdims=True) + EPS)
    return (y * g.astype(jnp.float32)).astype(x.dtype)


def _block_len(T, pref):
    return max(d for d in range(1, min(pref, T) + 1) if T % d == 0)


def _to_blocks(a, L):
    B, T, H = a.shape[:3]
    return a.reshape(B, T // L, L, H, -1).transpose(1, 0, 3, 2, 4)


def _from_blocks(a):
    NC, B, H, L, d = a.shape
    return a.transpose(1, 0, 3, 2, 4).reshape(B, NC * L, H, d)


def _mlstm(q, k, v, ig, lf, C0, n0, m0):
    T = q.shape[1]
    L = _block_len(T, CHUNK)
    causal = jnp.tril(jnp.ones((L, L), dtype=bool))
    xs = (_to_blocks(q, L), _to_blocks(k, L), _to_blocks(v, L),
          _to_blocks(ig[..., None], L)[..., 0], _to_blocks(lf[..., None], L)[..., 0])

    def step(carry, blk):
        C, n, m = carry
        qc, kc, vc, ic, fc = blk
        b = jnp.cumsum(fc, axis=-1)
        inter = b + m[..., None]
        D = jnp.where(causal, b[..., :, None] - b[..., None, :] + ic[..., None, :], NEG_BIG)
        mt = jnp.maximum(inter, jnp.max(D, axis=-1))
        a_inter = jnp.exp(inter - mt)
        S = jnp.einsum('bhtd,bhsd->bhts', qc, kc) * jnp.exp(D - mt[..., None])
        num = a_inter[..., None] * jnp.einsum('bhtd,bhde->bhte', qc, C) + jnp.einsum('bhts,bhse->bhte', S, vc)
        den = a_inter * jnp.einsum('bhtd,bhd->bht', qc, n) + jnp.sum(S, axis=-1)
        h = num / jnp.maximum(jnp.abs(den), jnp.exp(-mt))[..., None]
        wlast = b[..., -1:] - b + ic
        m_new = jnp.maximum(b[..., -1] + m, jnp.max(wlast, axis=-1))
        a_c = jnp.exp(b[..., -1] + m - m_new)
        ws = jnp.exp(wlast - m_new[..., None])
        C_new = a_c[..., None, None] * C + jnp.einsum('bhs,bhsd,bhse->bhde', ws, kc, vc)
        n_new = a_c[..., None] * n + jnp.einsum('bhs,bhsd->bhd', ws, kc)
        return (C_new, n_new, m_new), h

    (C, n, m), hs = lax.scan(step, (C0, n0, m0), xs)
    return _from_blocks(hs), C, n, m


def _hgrn2(q, kk, g, v, S0):
    T = q.shape[1]
    L = _block_len(T, HG_BLOCK)
    causal = jnp.tril(jnp.ones((L, L), dtype=bool))[:, :, None]
    xs = (_to_blocks(q, L), _to_blocks(kk, L), _to_blocks(g, L), _to_blocks(v, L))

    def step(S, blk):
        qc, kc, gc, vc = blk
        G = jnp.cumsum(gc, axis=2)
        o_inter = jnp.einsum('bhtd,bhde->bhte', qc * jnp.exp(G), S)
        diff = G[:, :, :, None, :] - G[:, :, None, :, :]
        decay = jnp.where(causal, jnp.exp(jnp.where(causal, diff, 0.0)), 0.0)
        A = jnp.einsum('bhtd,bhsd,bhtsd->bhts', qc, kc, decay)
        o = o_inter + jnp.einsum('bhts,bhse->bhte', A, vc)
        Gl = G[:, :, -1]
        S_new = jnp.exp(Gl)[..., None] * S + jnp.einsum('bhsd,bhse->bhde', kc * jnp.exp(Gl[:, :, None] - G), vc)
        return S_new, o

    S, os_ = lax.scan(step, S0, xs)
    return _from_blocks(os_), S


def _mixer(h, w_in, b_gate, lb, m_g, h_g, w_out, C0, n0, m0, S0):
    B, T, _ = h.shape
    z = h @ w_in
    mq, mk, mv, mo, mi, mf, hq, hf, hi, hg = jnp.split(z, np.cumsum(IN_SIZES)[:-1].tolist(), axis=-1)
    f32 = jnp.float32
    q = mq.astype(f32).reshape(B, T, M_HEADS, M_DK) * (M_DK ** -0.5)
    k = mk.astype(f32).reshape(B, T, M_HEADS, M_DK)
    v = mv.astype(f32).reshape(B, T, M_HEADS, M_DV)
    bg = b_gate.astype(f32)
    ig = mi.astype(f32) + bg[:M_HEADS]
    lf = jax.nn.log_sigmoid(mf.astype(f32) + bg[M_HEADS:])
    hm, C, n, m = _mlstm(q, k, v, ig, lf, C0.astype(f32), n0.astype(f32), m0.astype(f32))
    hm = _rmsnorm(hm, m_g.reshape(M_HEADS, M_DV)).reshape(B, T, M_WIDTH) * jax.nn.sigmoid(mo.astype(f32))
    fx = hf.astype(f32).reshape(B, T, H_HEADS, H_DK)
    lbr = lb.reshape(H_HEADS, H_DK)
    log_lb = jnp.log(jnp.maximum(lbr, LB_FLOOR))
    g = jnp.logaddexp(log_lb, jnp.log1p(-lbr) + jax.nn.log_sigmoid(fx))
    kk = (1.0 - lbr) * jax.nn.sigmoid(-fx)
    qh = jax.nn.silu(hq.astype(f32)).reshape(B, T, H_HEADS, H_DK)
    vh = hi.astype(f32).reshape(B, T, H_HEADS, H_DV)
    oh, S = _hgrn2(qh, kk, g, vh, S0.astype(f32))
    oh = _rmsnorm(oh, h_g.reshape(H_HEADS, H_DV)).reshape(B, T, H_WIDTH) * jax.nn.silu(hg.astype(f32))
    out = jnp.concatenate([hm, oh], axis=-1).astype(h.dtype) @ w_out
    return out, C, n, m, S


def _trunk(x, c, C0, n0, m0, S0, w_mod, b_mod, norm1_g, w_in, b_gate, lower_bounds,
           mlstm_norm_g, hgrn_norm_g, w_out, norm2_g, w_up, w_down, final_g):
    sm = jax.nn.softmax(lower_bounds.astype(jnp.float32), axis=0)
    lbs = jnp.cumsum(sm, axis=0) - sm[0:1]
    Cs, ns, ms, Ss = [], [], [], []
    for l in range(DEPTH):
        mod = (jax.nn.silu(c) @ w_mod[l] + b_mod[l])[:, None, :]
        sh1, sc1, ga1, sh2, sc2, ga2 = jnp.split(mod, 6, axis=-1)
        h = _rmsnorm(x, norm1_g[l]) * (1 + sc1) + sh1
        mix, C, n, m, S = _mixer(h, w_in[l], b_gate[l], lbs[l], mlstm_norm_g[l], hgrn_norm_g[l], w_out[l],
                                 C0[l], n0[l], m0[l], S0[l])
        x = x + ga1 * mix
        h = _rmsnorm(x, norm2_g[l]) * (1 + sc2) + sh2
        x = x + ga2 * (jnp.square(jax.nn.relu(h @ w_up[l])) @ w_down[l])
        Cs.append(C); ns.append(n); ms.append(m); Ss.append(S)
    y = _rmsnorm(x, final_g)
    return y, jnp.stack(Cs), jnp.stack(ns), jnp.stack(ms), jnp.stack(Ss)


def setup_inputs(seed: int = 0) -> dict:
    key = jax.random.key(seed)
    ks = jax.random.split(key, 24)
    nrm = jax.random.normal
    f32 = jnp.float32
    return {
        "x_prompt": nrm(ks[0], (BATCH, SEQ, D_MODEL), f32),
        "x_sample": nrm(ks[1], (DEC_BATCH, DEC_SEQ, D_MODEL), f32),
        "state_mlstm_C": 0.05 * nrm(ks[2], (DEPTH, DEC_BATCH, M_HEADS, M_DK, M_DV), f32),
        "state_mlstm_n": 0.5 * nrm(ks[3], (DEPTH, DEC_BATCH, M_HEADS, M_DK), f32),
        "state_mlstm_m": 0.5 * nrm(ks[4], (DEPTH, DEC_BATCH, M_HEADS), f32),
        "state_hgrn_S": 0.5 * nrm(ks[5], (DEPTH, DEC_BATCH, H_HEADS, H_DK, H_DV), f32),
        "c_prompt": nrm(ks[6], (BATCH, D_MODEL), f32),
        "c_sample": nrm(ks[7], (DEC_BATCH, D_MODEL), f32),
        "w_mod": 0.5 * D_MODEL ** -0.5 * nrm(ks[8], (DEPTH, D_MODEL, 6 * D_MODEL), f32),
        "b_mod": 0.02 * nrm(ks[9], (DEPTH, 6 * D_MODEL), f32),
        "norm1_g": 1.0 + 0.05 * nrm(ks[10], (DEPTH, D_MODEL), f32),
        "w_in": D_MODEL ** -0.5 * nrm(ks[11], (DEPTH, D_MODEL, IN_COLS), f32),
        "b_gate": jnp.concatenate([-1.0 + 0.1 * nrm(ks[12], (DEPTH, M_HEADS), f32),
                                   3.0 + 0.5 * nrm(ks[13], (DEPTH, M_HEADS), f32)], axis=-1),
        "lower_bounds": 0.5 * nrm(ks[14], (DEPTH, H_WIDTH), f32),
        "mlstm_norm_g": 1.0 + 0.05 * nrm(ks[15], (DEPTH, M_WIDTH), f32),
        "hgrn_norm_g": 1.0 + 0.05 * nrm(ks[16], (DEPTH, H_WIDTH), f32),
        "w_out": MIX_WIDTH ** -0.5 * nrm(ks[17], (DEPTH, MIX_WIDTH, D_MODEL), f32),
        "norm2_g": 1.0 + 0.05 * nrm(ks[18], (DEPTH, D_MODEL), f32),
        "w_up": D_MODEL ** -0.5 * nrm(ks[19], (DEPTH, D_MODEL, D_FF), f32),
        "w_down": D_FF ** -0.5 * nrm(ks[20], (DEPTH, D_FF, D_MODEL), f32),
        "final_g": 1.0 + 0.05 * nrm(ks[21], (D_MODEL,), f32),
    }


def reference(x_prompt, x_sample, state_mlstm_C, state_mlstm_n, state_mlstm_m, state_hgrn_S,
              c_prompt, c_sample, w_mod, b_mod, norm1_g, w_in, b_gate, lower_bounds,
              mlstm_norm_g, hgrn_norm_g, w_out, norm2_g, w_up, w_down, final_g):
    weights = (w_mod, b_mod, norm1_g, w_in, b_gate, lower_bounds, mlstm_norm_g, hgrn_norm_g,
               w_out, norm2_g, w_up, w_down, final_g)
    Bp = x_prompt.shape[0]
    f32 = jnp.float32
    zC = jnp.zeros((DEPTH, Bp, M_HEADS, M_DK, M_DV), f32)
    zn = jnp.zeros((DEPTH, Bp, M_HEADS, M_DK), f32)
    zm = jnp.zeros((DEPTH, Bp, M_HEADS), f32)
    zS = jnp.zeros((DEPTH, Bp, H_HEADS, H_DK, H_DV), f32)
    y_prompt, pC, pn, pm, pS = _trunk(x_prompt, c_prompt, zC, zn, zm, zS, *weights)
    y_sample, sC, sn, sm, sS = _trunk(x_sample, c_sample, state_mlstm_C, state_mlstm_n, state_mlstm_m,
                                      state_hgrn_S, *weights)
    return (y_prompt, y_sample, pC, pn, pm, pS, sC, sn, sm, sS)
```

```python
import numpy as np
import ml_dtypes
import concourse.bass as bass
import concourse.mybir as mybir
from concourse.bass_utils import run_bass_kernel_spmd

F32 = mybir.dt.float32
BF16 = mybir.dt.bfloat16
AF = mybir.ActivationFunctionType
ALU = mybir.AluOpType
AX = mybir.AxisListType

D = 4096
DC = 32
DEPTH = 2
M_HEADS, M_DK, M_DV = 4, 256, 512
H_HEADS, H_DK = 16, 128
D_FF = 16384
IN_COLS = 14344
C_MQ, C_MK, C_MV, C_MO, C_MG, C_HQ, C_HF, C_HI, C_HG = 0, 1024, 2048, 4096, 6144, 6152, 8200, 10248, 12296
EPS = 1e-6
NEG = -1e30
TT = 256
KC = 4
NRING = 8
NSAMP = 2
SLEN = 64


class Prog:
    ENG = ("pe", "act", "dve", "pool", "sp")

    def __init__(self, nc):
        self.nc = nc
        self.streams = {e: [] for e in self.ENG}
        self.res = {}
        self.dma_sems = []
        self.dma_rr = 0
        self.dma_last = {}
        self.named_sems = {}

    @staticmethod
    def key(r):
        if isinstance(r, (str, tuple)):
            return r
        return r.name

    def _deps(self, R, W):
        deps = []
        for r in R:
            st = self.res.setdefault(self.key(r), {"w": None, "r": {}, "rd": []})
            if st["w"] is not None:
                deps.append(st["w"])
        for w in W:
            st = self.res.setdefault(self.key(w), {"w": None, "r": {}, "rd": []})
            if st["w"] is not None:
                deps.append(st["w"])
            deps.extend(st["r"].values())
            deps.extend(st["rd"])
        return deps

    def _commit(self, ev, R, W):
        for r in R:
            st = self.res[self.key(r)]
            if ev[0] == "op":
                st["r"][ev[1]["eng"]] = ev
            else:
                st["rd"].append(ev)
        for w in W:
            st = self.res[self.key(w)]
            st["w"] = ev
            st["r"] = {}
            st["rd"] = []

    def op(self, eng, fn, R=(), W=()):
        deps = self._deps(R, W)
        ins = {"eng": eng, "fn": fn, "deps": deps, "kind": "op", "needed": False, "ticket": None}
        self.streams[eng].append(ins)
        self._commit(("op", ins), R, W)
        return ins

    def dma(self, eng, out, in_, R=(), W=(), semkey=None, **kw):
        deps = self._deps(R, W)
        if semkey is None:
            semkey = ("rr", self.dma_rr % 24)
            self.dma_rr += 1
        prev = self.dma_last.get(semkey)
        if prev is not None:
            deps.append(("dma", prev))
        cnt = (prev["count"] if prev else 0) + 16
        ins = {"eng": eng, "out": out, "in_": in_, "deps": deps, "kind": "dma", "semkey": semkey,
               "count": cnt, "kw": kw}
        self.dma_last[semkey] = ins
        self.streams[eng].append(ins)
        self._commit(("dma", ins), R, W)
        return ins

    def emit(self, final_waits):
        nc = self.nc
        for e in self.ENG:
            for ins in self.streams[e]:
                for kind, d in ins["deps"]:
                    if kind == "op":
                        d["needed"] = True
        for e in self.ENG:
            t = 0
            for ins in self.streams[e]:
                if ins["kind"] == "op" and ins["needed"]:
                    t += 1
                    ins["ticket"] = t
        import contextlib
        with contextlib.ExitStack() as es:
            esem = {e: es.enter_context(nc.semaphore("sem_" + e)) for e in self.ENG}
            dsem = {}
            for e in self.ENG:
                for ins in self.streams[e]:
                    if ins["kind"] == "dma" and ins["semkey"] not in dsem:
                        dsem[ins["semkey"]] = es.enter_context(nc.semaphore("dsem%d" % len(dsem)))
            block = es.enter_context(nc.Block())

            def run(engname, E):
                seen = {}
                def wait(sem_id, sem, val):
                    if seen.get(sem_id, 0) >= val:
                        return
                    seen[sem_id] = val
                    E.wait_ge(sem, val)
                for ins in self.streams[engname]:
                    for kind, d in ins["deps"]:
                        if kind == "op":
                            if d["eng"] == engname and engname == "pe":
                                continue
                            wait(d["eng"], esem[d["eng"]], d["ticket"])
                        else:
                            wait(d["semkey"], dsem[d["semkey"]], d["count"])
                    if ins["kind"] == "op":
                        r = ins["fn"](E)
                        if ins["needed"]:
                            r.then_inc(esem[engname], 1)
                    else:
                        E.dma_start(out=ins["out"], in_=ins["in_"], **ins["kw"]).then_inc(dsem[ins["semkey"]], 16)
                if engname == "sp":
                    for kind, d in final_waits:
                        if kind == "op":
                            wait(d["eng"], esem[d["eng"]], d["ticket"])
                        else:
                            wait(d["semkey"], dsem[d["semkey"]], d["count"])

            @block.tensor
            def _(E):
                run("pe", E)

            @block.scalar
            def _(E):
                run("act", E)

            @block.vector
            def _(E):
                run("dve", E)

            @block.gpsimd
            def _(E):
                run("pool", E)

            @block.sync
            def _(E):
                run("sp", E)


def build_program(SEQ):
    nc = bass.Bass("TRN2", target_bir_lowering=False)
    P = Prog(nc)
    import contextlib
    es = contextlib.ExitStack()

    def pbc(ap1d, n):
        return ap1d.rearrange("(o n) -> o n", o=1).broadcast_to([128, n])

    def dram(name, shape, kind="ExternalInput", dt=F32):
        return nc.dram_tensor(name, list(shape), dt, kind=kind).ap()

    def sb(name, shape, dt=F32):
        return es.enter_context(nc.sbuf_tensor(name, list(shape), dt))

    x_p = dram("x_p", [SEQ, D])
    x_s = dram("x_s", [NSAMP * SLEN, D])
    c_all = dram("c_all", [3 * DC, 128])
    sC_in = dram("sC_in", [DEPTH, NSAMP, M_HEADS, M_DK, M_DV])
    sn_in = dram("sn_in", [DEPTH, NSAMP, M_HEADS, M_DK])
    sm_in = dram("sm_in", [DEPTH, NSAMP, M_HEADS])
    sS_in = dram("sS_in", [DEPTH, NSAMP, H_HEADS, H_DK, H_DK])
    w_mod = dram("w_mod", [DEPTH, D, 6 * D])
    b_mod = dram("b_mod", [DEPTH * 192, 128])
    n1g = dram("norm1_g", [DEPTH * DC, 128])
    w_in = dram("w_in", [DEPTH, D, IN_COLS])
    b_gate = dram("b_gate", [DEPTH, 8])
    lowb = dram("lower_bounds", [DEPTH * 16, 128])
    mng = dram("mlstm_norm_g", [DEPTH, 2048])
    hng = dram("hgrn_norm_g", [DEPTH * 16, 128])
    w_out = dram("w_out", [DEPTH, D, D])
    n2g = dram("norm2_g", [DEPTH * DC, 128])
    w_up = dram("w_up", [DEPTH, D, D_FF])
    w_down = dram("w_down", [DEPTH, D_FF, D])
    fing = dram("final_g", [DC, 128])
    k_f32 = dram("k_f32", [128, 4, 128])
    k_bf = dram("k_bf", [128, 4, 128], dt=BF16)
    k_scan = dram("k_scan", [128, TT])

    y_p = dram("y_p", [SEQ, D], kind="ExternalOutput")
    y_s = dram("y_s", [NSAMP * SLEN, D], kind="ExternalOutput")
    pC = dram("pC", [DEPTH, M_HEADS, M_DK, M_DV], kind="ExternalOutput")
    pn = dram("pn", [DEPTH, M_HEADS, M_DK], kind="ExternalOutput")
    pm = dram("pm", [DEPTH, M_HEADS], kind="ExternalOutput")
    pS = dram("pS", [DEPTH, H_HEADS, H_DK, H_DK], kind="ExternalOutput")
    sC = dram("sC", [DEPTH, NSAMP, M_HEADS, M_DK, M_DV], kind="ExternalOutput")
    sn = dram("sn", [DEPTH, NSAMP, M_HEADS, M_DK], kind="ExternalOutput")
    sm = dram("sm", [DEPTH, NSAMP, M_HEADS], kind="ExternalOutput")
    sS = dram("sS", [DEPTH, NSAMP, H_HEADS, H_DK, H_DK], kind="ExternalOutput")

    xT = sb("xT", [128, DC, TT])
    bufA = sb("bufA", [128, DC, TT], BF16)
    bufB = sb("bufB", [128, DC, TT], BF16)
    ring = [sb("ring%d" % i, [128, KC, 512], BF16) for i in range(NRING)]
    kf = sb("kf", [128, 4, 128])
    kb = sb("kb", [128, 4, 128], BF16)
    kscan = sb("kscan", [128, TT])
    cst = sb("cst", [128, 4])
    modv = [sb("modv%d" % l, [128, 192, 3]) for l in range(DEPTH)]
    A1 = [sb("A1_%d" % l, [128, DC, 3]) for l in range(DEPTH)]
    A2 = [sb("A2_%d" % l, [128, DC, 3]) for l in range(DEPTH)]
    vecs = sb("vecs", [128, 2 * DC + 2 * DC + DC + 2 * 16 + 2 * 16 + 192 * 2 + 96])
    V_N1, V_N2, V_FIN, V_LB, V_HG, V_BM, V_C = 0, 64, 128, 160, 192, 224, 224 + 384
    vstage = sb("vstage", [128, 128])
    cT = sb("cT", [128, DC, 3], BF16)
    lbv = sb("lbv", [128, 2, 16, 3])
    rstd = sb("rstd", [128, TT])
    tmpn = sb("tmpn", [128, TT])
    bg_row = sb("bg_row", [128, DEPTH, 8])
    mg_row = sb("mg_row", [128, 512])
    qT = sb("qT", [128, 2, TT], BF16)
    kT = sb("kT", [128, 2, TT], BF16)
    k_tok = sb("k_tok", [128, 2, 256], BF16)
    v_tok = sb("v_tok", [128, 2, 512], BF16)
    og = sb("og", [128, 2, 512], BF16)
    gates = sb("gates", [128, 2, 8])
    lf = sb("lf", [128, 2, 4])
    Cst = [sb("Cst%d" % h, [128, 2, 512]) for h in range(M_HEADS)]
    Cbf = [sb("Cbf%d" % h, [128, 2, 512], BF16) for h in range(M_HEADS)]
    nst = [sb("nst%d" % h, [128, 2]) for h in range(M_HEADS)]
    nbf = [sb("nbf%d" % h, [128, 2], BF16) for h in range(M_HEADS)]
    mst = [sb("mst%d" % h, [128, 1]) for h in range(M_HEADS)]
    sm_small = sb("sm_small", [128, 24])
    Dm = sb("Dm", [128, 128])
    Em = sb("Em", [128, 128])
    Sm = sb("Sm", [128, 128], BF16)
    SmT = sb("SmT", [128, 128], BF16)
    diagu = sb("diagu", [128, 128])
    hsum = sb("hsum", [128, 512])
    numt = sb("numt", [128, 512])
    junk = sb("junk", [128, 512])
    GO = sb("GO", [128, 512])
    otok = sb("otok", [128, 512], BF16)
    ks = sb("ks", [128, 256], BF16)
    HH = 8
    hqT = sb("hqT", [128, HH, TT], BF16)
    sgT = sb("sgT", [128, HH, TT])
    hgT = sb("hgT", [128, HH, TT], BF16)
    hv_tok = sb("hv_tok", [128, 2, HH * 128], BF16)
    Sst = [sb("Sst%d" % h, [128, 128]) for h in range(H_HEADS)]
    Sbf = [sb("Sbf%d" % h, [128, 128], BF16) for h in range(H_HEADS)]
    fT = sb("fT", [128, TT])
    gT = sb("gT", [128, TT])
    GT = sb("GT", [128, TT])
    kkT = sb("kkT", [128, TT])
    eT = sb("eT", [128, TT])
    kx32 = sb("kx32", [128, TT])
    qxT = sb("qxT", [128, TT], BF16)
    kxT = sb("kxT", [128, TT], BF16)
    kdT = sb("kdT", [128, TT], BF16)
    eGl = sb("eGl", [128, TT // 16])
    kd_tok = sb("kd_tok", [128, 128], BF16)
    kdm = sb("kdm", [128, 8, 128], BF16)
    Am = sb("Am", [128, 128], BF16)
    oT = sb("oT", [128, 128])
    osq = sb("osq", [128, 128], BF16)
    orst = sb("orst", [128, 128])
    uT = [sb("uT%d" % i, [128, 4, TT], BF16) for i in range(2)]

    ps = [es.enter_context(nc.psum_tensor("ps%d" % i, [128, 512], F32)) for i in range(8)]

    IDF, TRIU, MNEG, ONESF = kf[:, 0, :], kf[:, 1, :], kf[:, 2, :], kf[:, 3, :]
    IDB, ONESB, HMASK, RMASK = kb[:, 0, :], kb[:, 1, :], kb[:, 2, :], kb[:, 3, :]
    CEPS, CONE, CZERO = cst[:, 0:1], cst[:, 1:2], cst[:, 2:3]

    def act(out, in_, func, R, W, bias=None, scale=1.0):
        kw = {}
        if bias is not None:
            kw["bias"] = bias
        return P.op("act", lambda E: E.activation(out=out, in_=in_, func=func, scale=scale, **kw), R, W)

    def ts(out, in0, s1, s2, op0, op1, R, W, eng="dve"):
        return P.op(eng, lambda E: E.tensor_scalar(out=out, in0=in0, scalar1=s1, scalar2=s2, op0=op0, op1=op1), R, W)

    def stt(out, in0, s, in1, op0, op1, R, W):
        return P.op("dve", lambda E: E.scalar_tensor_tensor(out=out, in0=in0, scalar=s, in1=in1, op0=op0, op1=op1), R, W)

    def tt(out, in0, in1, op, R, W, eng="dve"):
        return P.op(eng, lambda E: E.tensor_tensor(out=out, in0=in0, in1=in1, op=op), R, W)

    def red(out, in_, op, R, W):
        return P.op("dve", lambda E: E.tensor_reduce(out=out, in_=in_, axis=AX.X, op=op), R, W)

    def recip(out, in_, R, W):
        return P.op("dve", lambda E: E.reciprocal(out=out, in_=in_), R, W)

    def cp(out, in_, R, W, eng="dve"):
        if eng == "act":
            return P.op("act", lambda E: E.copy(out=out, in_=in_), R, W)
        return P.op(eng, lambda E: E.tensor_copy(out=out, in_=in_), R, W)

    def mm(out, lhsT, rhs, start, stop, R, W):
        return P.op("pe", lambda E: E.matmul(out, lhsT, rhs, start=start, stop=stop), R, W)

    def tr(out, in_, ident, R, W):
        return P.op("pe", lambda E: E.transpose(out, in_, ident), R, W)

    def memset(ap, v, W, eng="dve"):
        return P.op(eng, lambda E: E.memset(ap, v), (), W)

    P.dma("sp", kf[:], k_f32, W=[kf])
    P.dma("sp", kb[:], k_bf, W=[kb])
    P.dma("sp", kscan[:], k_scan, W=[kscan])
    memset(cst[:, 0:1], EPS, [cst])
    memset(cst[:, 1:2], 1.0, [cst])
    memset(cst[:, 2:3], 0.0, [cst])

    def load_vec_fm(src_rows, nrows, dst):
        P.dma("sp", vstage[:nrows, :], src_rows, W=[vstage])
        tr(ps[0][:, :nrows], vstage[:nrows, :], IDF[:nrows, :nrows], [vstage, kf], [ps[0]])
        cp(dst, ps[0][:, :nrows], [ps[0]], [vecs])

    load_vec_fm(n1g, 64, vecs[:, V_N1:V_N1 + 64])
    load_vec_fm(n2g, 64, vecs[:, V_N2:V_N2 + 64])
    load_vec_fm(fing, 32, vecs[:, V_FIN:V_FIN + 32])
    load_vec_fm(lowb, 32, vecs[:, V_LB:V_LB + 32])
    load_vec_fm(hng, 32, vecs[:, V_HG:V_HG + 32])
    for i in range(3):
        load_vec_fm(b_mod[i * 128:(i + 1) * 128, :], 128, vecs[:, V_BM + i * 128:V_BM + (i + 1) * 128])
    load_vec_fm(c_all, 96, vecs[:, V_C:V_C + 96])
    act(cT[:].rearrange("p c s -> p s c"), vecs[:, V_C:V_C + 96].rearrange("p (s c) -> p s c", s=3), AF.Silu, [vecs], [cT])
    l0 = vecs[:, V_LB:V_LB + 16]
    l1 = vecs[:, V_LB + 16:V_LB + 32]
    tt(tmpn[:, 0:16], l1, l0, ALU.subtract, [vecs], [tmpn])
    act(lbv[:, 1, :, 0], tmpn[:, 0:16], AF.Sigmoid, [tmpn], [lbv])
    act(tmpn[:, 16:32], tmpn[:, 0:16], AF.Sigmoid, [tmpn], [tmpn], scale=-1.0)
    tt(lbv[:, 0, :, 0], tmpn[:, 16:32], tmpn[:, 16:32], ALU.subtract, [tmpn], [lbv])
    for l in range(DEPTH):
        ts(lbv[:, l, :, 1], lbv[:, l, :, 0], -1.0, 1.0, ALU.mult, ALU.add, [lbv], [lbv])
        ts(lbv[:, l, :, 2], lbv[:, l, :, 0], 1.0, -1.0, ALU.mult, ALU.add, [lbv], [lbv])
    P.dma("sp", bg_row[:].rearrange("p l g -> p (l g)"),
          b_gate.rearrange("(o l) g -> o (l g)", o=1).broadcast_to([128, DEPTH * 8]), W=[bg_row])

    ring_state = {"i": 0}

    def load_w(Wl, kc0, nk, col0, ncols):
        i = ring_state["i"]
        ring_state["i"] += 1
        slot = i % NRING
        src = Wl.rearrange("(kc p) c -> p kc c", p=128)[:, kc0:kc0 + nk, col0:col0 + ncols]
        P.dma("pool", ring[slot][:, :nk, :ncols], src, W=[ring[slot]], semkey=("ring", slot))
        return ring[slot]

    bank_rr = {"i": 0}

    def proj(Wl, nkc, col0, ncols, mode, actT, actR, blocks, evac):
        base = 4 * (bank_rr["i"] % 2)
        bank_rr["i"] += 1
        if mode == "FM":
            nout = (ncols + 127) // 128
        else:
            nout = len(blocks)
        assert nout <= 4
        for kg in range(0, nkc, KC):
            nk = min(KC, nkc - kg)
            slot = load_w(Wl, kg, nk, col0, ncols)
            for kk in range(nk):
                kc = kg + kk
                a = actT(kc)
                for j in range(nout):
                    if mode == "FM":
                        cw = min(128, ncols - j * 128)
                        ntok = a.shape[-1]
                        mm(ps[base + j][:cw, :ntok], slot[:, kk, j * 128:j * 128 + cw], a,
                           kc == 0, kc == nkc - 1, [slot] + actR(kc), [ps[base + j]])
                    else:
                        c0, n = blocks[j]
                        mm(ps[base + j][:n, :ncols], a[:, c0:c0 + n], slot[:, kk, :ncols],
                           kc == 0, kc == nkc - 1, [slot] + actR(kc), [ps[base + j]])
        for j in range(nout):
            if mode == "FM":
                cw = min(128, ncols - j * 128)
                evac(j, ps[base + j][:cw, :], ps[base + j])
            else:
                c0, n = blocks[j]
                evac(j, ps[base + j][:n, :ncols], ps[base + j])

    for l in range(DEPTH):
        for g in range(48):
            def ev(j, pap, pres, l=l, g=g):
                ch = g * 4 + j
                bm = vecs[:, V_BM + l * 192 + ch:V_BM + l * 192 + ch + 1]
                ts(modv[l][:, ch, :], pap[:, :3], bm, None, ALU.add, ALU.bypass, [pres, vecs], [modv[l]])
            proj(w_mod[l], DC, g * 512, 512, "FM", lambda kc: cT[:, kc, :], lambda kc: [cT], None, ev)
        for (Ax, voff, sck) in ((A1, V_N1, 1), (A2, V_N2, 4)):
            for s in range(3):
                stt(Ax[l][:, :, s], modv[l][:, sck * 32:(sck + 1) * 32, s], 1.0, vecs[:, voff + l * 32:voff + (l + 1) * 32],
                    ALU.add, ALU.mult, [modv[l], vecs], [Ax[l]])

    def modk(l, kind, c, s):
        return modv[l][:, kind * 32 + c, s:s + 1]

    def adaln(l, which, ntok, segs):
        act(bufA[:, :, :ntok], xT[:, :, :ntok], AF.Square, [xT], [bufA])
        for c in range(DC):
            mm(ps[0][:, :ntok], ONESB, bufA[:, c, :ntok], c == 0, c == DC - 1, [bufA, kb], [ps[0]])
        act(tmpn[:, :ntok], ps[0][:, :ntok], AF.Sqrt, [ps[0], cst], [tmpn], bias=CEPS, scale=1.0 / D)
        recip(rstd[:, :ntok], tmpn[:, :ntok], [tmpn], [rstd])
        if which == 0:
            Ax, shk = A1, 0
        elif which == 1:
            Ax, shk = A2, 3
        for c in range(DC):
            for (c0, n, s) in segs:
                stt(tmpn[:, c0:c0 + n], xT[:, c, c0:c0 + n], Ax[l][:, c, s:s + 1], rstd[:, c0:c0 + n],
                    ALU.mult, ALU.mult, [xT, Ax[l], rstd], [tmpn])
                act(bufA[:, c, c0:c0 + n], tmpn[:, c0:c0 + n], AF.Identity, [tmpn, modv[l]], [("bufA", c)],
                    bias=modk(l, shk, c, s))
        return

    def bufA_R(kc):
        return [bufA, ("bufA", kc)]

    def mlstm_chunk(l, h, c0, n, bi, first_zero_state):
        sml = sm_small
        col = lambda i: sml[:n, i:i + 1]
        colP = lambda i: sml[:, i:i + 1]
        ig = gates[:n, bi, h:h + 1]
        lfv = lf[:n, bi, h:h + 1]
        mm(ps[1][:n, 0:1], TRIU[:n, :n], lfv, True, True, [kf, lf], [ps[1]])
        mm(ps[1][:, 1:2], ONESF[:n, :], lfv, True, True, [kf, lf], [ps[1]])
        cp(col(0), ps[1][:n, 0:1], [ps[1]], [sml])
        cp(colP(1), ps[1][:, 1:2], [ps[1]], [sml])
        tt(col(2), ig, col(0), ALU.subtract, [gates, sml], [sml])
        ts(diagu[:n, :n], IDF[:n, :n], col(2), None, ALU.mult, ALU.bypass, [kf, sml], [diagu])
        mm(ps[1][:, 128:128 + n], ONESF[:n, :], diagu[:n, :n], True, True, [kf, diagu], [ps[1]])
        stt(Dm[:n, :n], ps[1][:n, 128:128 + n], col(0), MNEG[:n, :n], ALU.add, ALU.add, [ps[1], sml, kf], [Dm])
        red(col(3), Dm[:n, :n], ALU.max, [Dm], [sml])
        red(colP(4), ps[1][:, 128:128 + n], ALU.max, [ps[1]], [sml])
        tt(col(5), col(0), mst[h][:n, :], ALU.add, [sml, mst[h]], [sml])
        tt(col(6), col(5), col(3), ALU.max, [sml], [sml])
        ts(col(7), col(6), -1.0, None, ALU.mult, ALU.bypass, [sml], [sml])
        act(col(8), col(5), AF.Exp, [sml], [sml], bias=col(7))
        act(Em[:n, :n], Dm[:n, :n], AF.Exp, [Dm, sml], [Em], bias=col(7))
        for c in range(2):
            mm(ps[0][:n, :n], qT[:, c, c0:c0 + n], kT[:, c, c0:c0 + n], c == 0, c == 1, [qT, kT], [ps[0]])
        tt(Sm[:n, :n], ps[0][:n, :n], Em[:n, :n], ALU.mult, [ps[0], Em], [Sm])
        red(col(9), Sm[:n, :n], ALU.add, [Sm], [sml])
        pst = ps[4][:].bitcast(BF16)
        tr(pst[:n, :n], Sm[:n, :n], IDB[:n, :n], [Sm, kb], [ps[4]])
        cp(SmT[:n, :n], pst[:n, :n], [ps[4]], [SmT], eng="act")
        for c in range(2):
            mm(ps[2][:n, :], qT[:, c, c0:c0 + n], Cbf[h][:, c, :], c == 0, c == 1, [qT, Cbf[h]], [ps[2]])
        for c in range(2):
            mm(ps[5][:n, 0:1], qT[:, c, c0:c0 + n], nbf[h][:, c:c + 1], c == 0, c == 1, [qT, nbf[h]], [ps[5]])
        mm(ps[3][:n, :], SmT[:n, :n], v_tok[:n, bi, :], True, True, [SmT, v_tok], [ps[3]])
        P.op("act", lambda E: E.activation(out=hsum[:n, :], in_=ps[2][:n, :], func=AF.Copy, scale=col(8)), [ps[2], sml], [hsum])
        tt(numt[:n, :], ps[3][:n, :], hsum[:n, :], ALU.add, [ps[3], hsum], [numt])
        stt(col(10), ps[5][:n, 0:1], col(8), col(9), ALU.mult, ALU.add, [ps[5], sml], [sml])
        ts(col(20), col(10), -1.0, None, ALU.mult, ALU.bypass, [sml], [sml])
        tt(col(10), col(10), col(20), ALU.max, [sml], [sml])
        act(col(11), col(6), AF.Exp, [sml], [sml], scale=-1.0)
        tt(col(10), col(10), col(11), ALU.max, [sml], [sml])
        recip(col(12), col(10), [sml], [sml])
        act(junk[:n, :], numt[:n, :], AF.Square, [numt], [junk])
        red(col(13), junk[:n, :], ALU.add, [junk], [sml])
        tt(col(14), col(12), col(12), ALU.mult, [sml], [sml])
        tt(col(14), col(14), col(13), ALU.mult, [sml], [sml])
        act(col(15), col(14), AF.Sqrt, [sml, cst], [sml], bias=CEPS[:n, :], scale=1.0 / M_DV)
        recip(col(15), col(15), [sml], [sml])
        tt(col(15), col(15), col(12), ALU.mult, [sml], [sml])
        tt(GO[:n, :], og[:n, bi, :], mg_row[:n, :], ALU.mult, [og, mg_row], [GO])
        stt(otok[:n, :], numt[:n, :], col(15), GO[:n, :], ALU.mult, ALU.mult, [numt, sml, GO], [otok])
        for j in range(4):
            tr(pst[:, 512 + j * 128:512 + j * 128 + n], otok[:n, j * 128:(j + 1) * 128], IDB[:n, :n], [otok, kb], [ps[4]])
        cp(bufB[:, h * 4:(h + 1) * 4, c0:c0 + n], pst[:, 512:1024].rearrange("p (j t) -> p j t", j=4)[:, :, :n],
           [ps[4]], [bufB], eng="act")
        tt(colP(16), mst[h][:, :], colP(4), ALU.max, [mst[h], sml], [sml])
        ts(colP(17), colP(16), -1.0, None, ALU.mult, ALU.bypass, [sml], [sml])
        act(colP(18), mst[h][:, :], AF.Exp, [mst[h], sml], [sml], bias=colP(17))
        act(col(19), col(2), AF.Exp, [sml], [sml], bias=col(17))
        tt(mst[h][:, :], colP(16), colP(1), ALU.add, [sml], [mst[h]])
        ts(ks[:n, :], k_tok[:n, bi, :], col(19), None, ALU.mult, ALU.bypass, [k_tok, sml], [ks])
        for c in range(2):
            mm(ps[6 + c][:, :], ks[:n, c * 128:(c + 1) * 128], v_tok[:n, bi, :], True, True, [ks, v_tok], [ps[6 + c]])
            mm(ps[5][:, 8 + c:9 + c], ks[:n, c * 128:(c + 1) * 128], ONESB[:n, 0:1], True, True, [ks, kb], [ps[5]])
        for c in range(2):
            stt(Cst[h][:, c, :], Cst[h][:, c, :], colP(18), ps[6 + c][:, :], ALU.mult, ALU.add, [Cst[h], sml, ps[6 + c]], [Cst[h]])
            cp(Cbf[h][:, c, :], Cst[h][:, c, :], [Cst[h]], [Cbf[h]], eng="act")
        stt(nst[h][:, :], nst[h][:, :], colP(18), ps[5][:, 8:10], ALU.mult, ALU.add, [nst[h], sml, ps[5]], [nst[h]])
        cp(nbf[h][:, :], nst[h][:, :], [nst[h]], [nbf[h]])

    def mlstm_state_load(l, h, run):
        if run["zero"]:
            memset(Cst[h][:], 0.0, [Cst[h]])
            memset(Cbf[h][:], 0.0, [Cbf[h]])
            memset(nst[h][:], 0.0, [nst[h]])
            memset(nbf[h][:], 0.0, [nbf[h]])
            memset(mst[h][:], 0.0, [mst[h]])
            return
        Csrc, nsrc, msrc = run["C"], run["n"], run["m"]
        P.dma("sp", Cst[h][:], Csrc[l, h].rearrange("(c p) e -> p c e", p=128), R=[(run["key"], l, h, "C")], W=[Cst[h]])
        P.dma("sp", nst[h][:], nsrc[l, h].rearrange("(c p) -> p c", p=128), R=[(run["key"], l, h, "n")], W=[nst[h]],
              allow_slow_non_contiguous=True)
        P.dma("sp", mst[h][:], pbc(msrc[l, h:h + 1], 1), R=[(run["key"], l, h, "m")], W=[mst[h]])
        cp(Cbf[h][:], Cst[h][:], [Cst[h]], [Cbf[h]], eng="act")
        cp(nbf[h][:], nst[h][:], [nst[h]], [nbf[h]])

    out_dmas = []

    def mlstm_state_store(l, h, run):
        Cd, nd, md = run["Co"], run["no"], run["mo"]
        out_dmas.append(P.dma("sp", Cd[l, h].rearrange("(c p) e -> p c e", p=128), Cst[h][:], R=[Cst[h]], W=[(run["okey"], l, h, "C")]))
        out_dmas.append(P.dma("sp", nd[l, h].rearrange("(c p) -> p c", p=128), nst[h][:], R=[nst[h]], W=[(run["okey"], l, h, "n")],
                              allow_slow_non_contiguous=True))
        out_dmas.append(P.dma("sp", md[l, h:h + 1].rearrange("(o n) -> o n", o=1), mst[h][0:1, :], R=[mst[h]], W=[(run["okey"], l, h, "m")]))

    def mlstm(l, tile):
        ntok, blocks, runs = tile["ntok"], tile["blocks"], tile["runs"]
        Wl = w_in[l]
        hT = lambda kc: bufA[:, kc, :ntok]
        def ev_g(b, pap, pres):
            c0, n = blocks[b]
            tt(gates[:n, b, :], pap, bg_row[:n, l, :], ALU.add, [pres, bg_row], [gates])
            act(lf[:n, b, :], gates[:n, b, 4:8], AF.Exp, [gates], [lf], scale=-1.0)
            act(lf[:n, b, :], lf[:n, b, :], AF.Ln, [lf, cst], [lf], bias=CONE[:n, :])
            ts(lf[:n, b, :], lf[:n, b, :], -1.0, None, ALU.mult, ALU.bypass, [lf], [lf])
        proj(Wl, DC, C_MG, 8, "TM", hT, bufA_R, blocks, ev_g)
        for h in range(M_HEADS):
            P.dma("sp", mg_row[:], pbc(mng[l, h * 512:(h + 1) * 512], 512), W=[mg_row])
            def ev_q(j, pap, pres):
                ts(qT[:, j, :ntok], pap[:, :ntok], M_DK ** -0.5, None, ALU.mult, ALU.bypass, [pres], [qT])
            def ev_k(j, pap, pres):
                cp(kT[:, j, :ntok], pap[:, :ntok], [pres], [kT], eng="act")
            def ev_kt(b, pap, pres):
                cp(k_tok[:blocks[b][1], b, :], pap, [pres], [k_tok])
            def ev_v(b, pap, pres):
                cp(v_tok[:blocks[b][1], b, :], pap, [pres], [v_tok], eng="act")
            def ev_o(b, pap, pres):
                act(og[:blocks[b][1], b, :], pap, AF.Sigmoid, [pres], [og])
            proj(Wl, DC, C_MQ + h * 256, 256, "FM", hT, bufA_R, None, ev_q)
            proj(Wl, DC, C_MK + h * 256, 256, "FM", hT, bufA_R, None, ev_k)
            proj(Wl, DC, C_MK + h * 256, 256, "TM", hT, bufA_R, blocks, ev_kt)
            proj(Wl, DC, C_MV + h * 512, 512, "TM", hT, bufA_R, blocks, ev_v)
            proj(Wl, DC, C_MO + h * 512, 512, "TM", hT, bufA_R, blocks, ev_o)
            for run in runs:
                mlstm_state_load(l, h, run)
                for bi in run["blocks"]:
                    c0, n = blocks[bi]
                    mlstm_chunk(l, h, c0, n, bi, run["zero"])
                mlstm_state_store(l, h, run)

    def hgrn_head(l, hh, h, tile, half):
        ntok, blocks, runs = tile["ntok"], tile["blocks"], tile["runs"]
        nb = ntok // 16
        lb, oml, moml = lbv[:, l, h, 0:1], lbv[:, l, h, 1:2], lbv[:, l, h, 2:3]
        sg = sgT[:, hh, :ntok]
        ts(fT[:, :ntok], sg, oml, lb, ALU.mult, ALU.add, [sgT, lbv], [fT])
        act(gT[:, :ntok], fT[:, :ntok], AF.Ln, [fT], [gT])
        P.op("dve", lambda E: E.tensor_tensor_scan(out=GT[:, :ntok], data0=kscan[:, :ntok], data1=gT[:, :ntok],
                                                   initial=0.0, op0=ALU.mult, op1=ALU.add), [kscan, gT], [GT])
        ts(kkT[:, :ntok], sg, moml, oml, ALU.mult, ALU.add, [sgT, lbv], [kkT])
        act(eT[:, :ntok], GT[:, :ntok], AF.Exp, [GT], [eT])
        tt(qxT[:, :ntok], hqT[:, hh, :ntok], eT[:, :ntok], ALU.mult, [hqT, eT], [qxT])
        act(eT[:, :ntok], GT[:, :ntok], AF.Exp, [GT], [eT], scale=-1.0)
        tt(kx32[:, :ntok], kkT[:, :ntok], eT[:, :ntok], ALU.mult, [kkT, eT], [kx32])
        cp(kxT[:, :ntok], kx32[:, :ntok], [kx32], [kxT])
        GTv = GT[:, :ntok].rearrange("p (b k) -> p b k", k=16)
        act(eGl[:, :nb], GTv[:, :, 15], AF.Exp, [GT], [eGl])
        tt(kdT[:, :ntok].rearrange("p (b k) -> p b k", k=16), kx32[:, :ntok].rearrange("p (b k) -> p b k", k=16),
           eGl[:, :nb].unsqueeze(2).broadcast_to([128, nb, 16]), ALU.mult, [kx32, eGl], [kdT])
        set0 = 4 * (h % 2)
        pA, pR, pU = ps[set0], ps[set0 + 1], (ps[set0 + 2], ps[set0 + 3])
        pAb = pA[:].bitcast(BF16)
        for run in runs:
            if run["zero"]:
                memset(Sst[h][:], 0.0, [Sst[h]])
                memset(Sbf[h][:], 0.0, [Sbf[h]])
            else:
                P.dma("sp", Sst[h][:], run["S"][l, h], R=[(run["key"], l, h, "S")], W=[Sst[h]])
                cp(Sbf[h][:], Sst[h][:], [Sst[h]], [Sbf[h]], eng="act")
            for bi in run["blocks"]:
                c0, n = blocks[bi]
                nsb = n // 16
                tr(pAb[:n, 512:640], kdT[:, c0:c0 + n], IDB, [kdT, kb], [pA])
                cp(kd_tok[:n, :], pAb[:n, 512:640], [pA], [kd_tok], eng="act")
                tt(kdm[:n, :nsb, :], kd_tok[:n, :].unsqueeze(1).broadcast_to([n, nsb, 128]),
                   RMASK[:n, :nsb].unsqueeze(2).broadcast_to([n, nsb, 128]), ALU.mult, [kd_tok, kb], [kdm])
                mm(pA[:n, :n], kxT[:, c0:c0 + n], qxT[:, c0:c0 + n], True, True, [kxT, qxT], [pA])
                tt(Am[:n, :n], pA[:n, :n], HMASK[:n, :n], ALU.mult, [pA, kb], [Am])
                mm(pR[:, 0:n], hv_tok[:n, bi, hh * 128:(hh + 1) * 128], Am[:n, :n], True, True, [hv_tok, Am], [pR])
                for sbk in range(nsb):
                    t0 = c0 + sbk * 16
                    mm(pR[:, 128 + sbk * 16:128 + sbk * 16 + 16], Sbf[h][:, :], qxT[:, t0:t0 + 16], True, True,
                       [Sbf[h], qxT], [pR])
                    pu = pU[sbk % 2]
                    mm(pu[:, 0:128], kdm[:n, sbk, :], hv_tok[:n, bi, hh * 128:(hh + 1) * 128], True, True,
                       [kdm, hv_tok], [pu])
                    gi = (c0 // 16) + sbk
                    stt(Sst[h][:, :], Sst[h][:, :], eGl[:, gi:gi + 1], pu[:, 0:128], ALU.mult, ALU.add,
                        [Sst[h], eGl, pu], [Sst[h]])
                    cp(Sbf[h][:, :], Sst[h][:, :], [Sst[h]], [Sbf[h]], eng="act")
                cp(oT[:, :n], pR[:, 0:n], [pR], [oT], eng="act")
                tt(oT[:, :n], oT[:, :n], pR[:, 128:128 + n], ALU.add, [oT, pR], [oT])
                act(osq[:, :n], oT[:, :n], AF.Square, [oT], [osq])
                mm(pR[:, 256:256 + n], ONESB, osq[:, :n], True, True, [kb, osq], [pR])
                act(orst[:, :n], pR[:, 256:256 + n], AF.Sqrt, [pR, cst], [orst], bias=CEPS, scale=1.0 / H_DK)
                recip(orst[:, :n], orst[:, :n], [orst], [orst])
                stt(oT[:, :n], oT[:, :n], vecs[:, V_HG + l * 16 + h:V_HG + l * 16 + h + 1], orst[:, :n], ALU.mult, ALU.mult,
                    [oT, vecs, orst], [oT])
                tt(bufB[:, 16 + h, c0:c0 + n], oT[:, :n], hgT[:, hh, c0:c0 + n], ALU.mult, [oT, hgT], [bufB])
            out_dmas.append(P.dma("sp", run["So"][l, h], Sst[h][:], R=[Sst[h]], W=[(run["okey"], l, h, "S")]))

    def hgrn(l, tile):
        ntok, blocks = tile["ntok"], tile["blocks"]
        Wl = w_in[l]
        hT = lambda kc: bufA[:, kc, :ntok]
        for half in range(2):
            for g in range(2):
                hb = half * HH + g * 4
                def ev_q(j, pap, pres, g=g):
                    act(hqT[:, g * 4 + j, :ntok], pap[:, :ntok], AF.Silu, [pres], [hqT])
                def ev_f(j, pap, pres, g=g):
                    act(sgT[:, g * 4 + j, :ntok], pap[:, :ntok], AF.Sigmoid, [pres], [sgT])
                def ev_g(j, pap, pres, g=g):
                    act(hgT[:, g * 4 + j, :ntok], pap[:, :ntok], AF.Silu, [pres], [hgT])
                def ev_v(b, pap, pres, g=g):
                    cp(hv_tok[:blocks[b][1], b, g * 512:(g + 1) * 512], pap, [pres], [hv_tok])
                proj(Wl, DC, C_HQ + hb * 128, 512, "FM", hT, bufA_R, None, ev_q)
                proj(Wl, DC, C_HF + hb * 128, 512, "FM", hT, bufA_R, None, ev_f)
                proj(Wl, DC, C_HG + hb * 128, 512, "FM", hT, bufA_R, None, ev_g)
                proj(Wl, DC, C_HI + hb * 128, 512, "TM", hT, bufA_R, blocks, ev_v)
            for hh in range(HH):
                hgrn_head(l, hh, half * HH + hh, tile, half)

    def out_proj(l, tile):
        ntok, segs = tile["ntok"], tile["segs"]
        for g in range(8):
            def ev(j, pap, pres, g=g):
                c = g * 4 + j
                for (c0, n, s) in segs:
                    stt(xT[:, c, c0:c0 + n], pap[:, c0:c0 + n], modk(l, 2, c, s), xT[:, c, c0:c0 + n], ALU.mult, ALU.add,
                        [pres, modv[l], xT], [xT])
            proj(w_out[l], DC, g * 512, 512, "FM", lambda kc: bufB[:, kc, :ntok], lambda kc: [bufB], None, ev)

    def ffn(l, tile):
        ntok, segs = tile["ntok"], tile["segs"]
        for hbk in range(D_FF // 512):
            u = uT[hbk % 2]
            def ev_u(j, pap, pres, u=u):
                act(junk[:, :ntok], pap[:, :ntok], AF.Relu, [pres], [junk])
                tt(u[:, j, :ntok], junk[:, :ntok], junk[:, :ntok], ALU.mult, [junk], [u])
            proj(w_up[l], DC, hbk * 512, 512, "FM", lambda kc: bufA[:, kc, :ntok], bufA_R, None, ev_u)
            for g in range(8):
                def ev_d(j, pap, pres, g=g):
                    c = g * 4 + j
                    for (c0, n, s) in segs:
                        stt(xT[:, c, c0:c0 + n], pap[:, c0:c0 + n], modk(l, 5, c, s), xT[:, c, c0:c0 + n], ALU.mult, ALU.add,
                            [pres, modv[l], xT], [xT])
                proj(w_down[l][hbk * 512:(hbk + 1) * 512, :], 4, g * 512, 512, "FM", lambda kc, u=u: u[:, kc, :ntok],
                     lambda kc, u=u: [u], None, ev_d)

    stage = bufB[:].rearrange("p c t -> p (c t)").bitcast(F32)
    assert DC * TT // 2 >= D

    def load_x(tile):
        src = tile["x"]
        for bi, (c0, n) in enumerate([(i * 128, 128) for i in range(tile["ntok"] // 128)]):
            P.dma("sp", stage[:, :D], src[c0:c0 + 128, :], W=[bufB])
            for c in range(DC):
                bank = ps[c % 8]
                tr(bank[:, 0:128], stage[:, c * 128:(c + 1) * 128], IDF, [bufB, kf], [bank])
                cp(xT[:, c, c0:c0 + 128], bank[:, 0:128], [bank], [xT], eng=("act" if c % 2 else "dve"))

    def store_y(tile):
        ntok = tile["ntok"]
        act(bufA[:, :, :ntok], xT[:, :, :ntok], AF.Square, [xT], [bufA])
        for c in range(DC):
            mm(ps[0][:, :ntok], ONESB, bufA[:, c, :ntok], c == 0, c == DC - 1, [bufA, kb], [ps[0]])
        act(tmpn[:, :ntok], ps[0][:, :ntok], AF.Sqrt, [ps[0], cst], [tmpn], bias=CEPS, scale=1.0 / D)
        recip(rstd[:, :ntok], tmpn[:, :ntok], [tmpn], [rstd])
        for c in range(DC):
            stt(xT[:, c, :ntok], xT[:, c, :ntok], vecs[:, V_FIN + c:V_FIN + c + 1], rstd[:, :ntok], ALU.mult, ALU.mult,
                [xT, vecs, rstd], [xT])
        for bi in range(ntok // 128):
            c0 = bi * 128
            for c in range(DC):
                bank = ps[c % 8]
                tr(bank[:, 0:128], xT[:, c, c0:c0 + 128], IDF, [xT, kf], [bank])
                cp(stage[:, c * 128:(c + 1) * 128], bank[:, 0:128], [bank], [bufB], eng=("act" if c % 2 else "dve"))
            out_dmas.append(P.dma("sp", tile["y"][c0:c0 + 128, :], stage[:, :D], R=[bufB], W=[("ydram", id(tile), bi)]))

    tiles = []
    nt = SEQ // TT
    for i in range(nt):
        run = {"zero": i == 0, "blocks": list(range(TT // 128)), "key": "pstate", "okey": "pstate",
               "C": pC, "n": pn, "m": pm, "S": pS, "Co": pC, "no": pn, "mo": pm, "So": pS}
        tiles.append({"ntok": TT, "x": x_p[i * TT:(i + 1) * TT, :], "y": y_p[i * TT:(i + 1) * TT, :],
                      "segs": [(0, TT, 0)], "blocks": [(j * 128, 128) for j in range(TT // 128)], "runs": [run]})
    sruns = []
    for s in range(NSAMP):
        sruns.append({"zero": False, "blocks": [s], "key": "sstate_in", "okey": "sstate_out",
                      "C": sC_in[:, s], "n": sn_in[:, s], "m": sm_in[:, s], "S": sS_in[:, s],
                      "Co": sC[:, s], "no": sn[:, s], "mo": sm[:, s], "So": sS[:, s]})
    tiles.append({"ntok": NSAMP * SLEN, "x": x_s, "y": y_s, "segs": [(s * SLEN, SLEN, 1 + s) for s in range(NSAMP)],
                  "blocks": [(s * SLEN, SLEN) for s in range(NSAMP)], "runs": sruns})

    for tile in tiles:
        load_x(tile)
        for l in range(DEPTH):
            adaln(l, 0, tile["ntok"], tile["segs"])
            mlstm(l, tile)
            hgrn(l, tile)
            out_proj(l, tile)
            adaln(l, 1, tile["ntok"], tile["segs"])
            ffn(l, tile)
        store_y(tile)

    P.emit([("dma", d) for d in out_dmas])
    es.close()
    return nc


def _consts():
    kf = np.zeros((128, 4, 128), np.float32)
    kf[:, 0, :] = np.eye(128)
    s = np.arange(128)[:, None]
    t = np.arange(128)[None, :]
    kf[:, 1, :] = (s <= t)
    kf[:, 2, :] = np.where(t <= s, 0.0, NEG)
    kf[:, 3, :] = 1.0
    kb = np.zeros((128, 4, 128), np.float32)
    kb[:, 0, :] = np.eye(128)
    kb[:, 1, :] = 1.0
    kb[:, 2, :] = ((s // 16) == (t // 16)) & (s <= t)
    kb[:, 3, :8] = ((np.arange(128)[:, None] // 16) == np.arange(8)[None, :])
    ksc = np.ones((128, TT), np.float32)
    ksc[:, ::16] = 0.0
    return kf, kb.astype(ml_dtypes.bfloat16), ksc


_CACHE = {}


def kernel(x_prompt, x_sample, state_mlstm_C, state_mlstm_n, state_mlstm_m, state_hgrn_S,
           c_prompt, c_sample, w_mod, b_mod, norm1_g, w_in, b_gate, lower_bounds,
           mlstm_norm_g, hgrn_norm_g, w_out, norm2_g, w_up, w_down, final_g, _ncores=8):
    f = lambda a: np.ascontiguousarray(np.asarray(a), dtype=np.float32)
    x_prompt, x_sample = f(x_prompt), f(x_sample)
    B, SEQ = x_prompt.shape[0], x_prompt.shape[1]
    ncores = _ncores
    if SEQ not in _CACHE:
        _CACHE[SEQ] = build_program(SEQ)
    nc = _CACHE[SEQ]
    kf, kb, ksc = _consts()
    shared = {
        "w_mod": f(w_mod), "b_mod": f(b_mod).reshape(DEPTH * 192, 128), "norm1_g": f(norm1_g).reshape(DEPTH * DC, 128),
        "w_in": f(w_in), "b_gate": f(b_gate), "lower_bounds": f(lower_bounds).reshape(DEPTH * 16, 128),
        "mlstm_norm_g": f(mlstm_norm_g), "hgrn_norm_g": f(hgrn_norm_g).reshape(DEPTH * 16, 128),
        "w_out": f(w_out), "norm2_g": f(norm2_g).reshape(DEPTH * DC, 128), "w_up": f(w_up), "w_down": f(w_down),
        "final_g": f(final_g).reshape(DC, 128), "k_f32": kf, "k_bf": kb, "k_scan": ksc,
    }
    sC_, sn_, sm_, sS_ = f(state_mlstm_C), f(state_mlstm_n), f(state_mlstm_m), f(state_hgrn_S)
    c_prompt, c_sample = f(c_prompt), f(c_sample)
    in_maps = []
    for c in range(ncores):
        pb = c % B
        ss = [(NSAMP * c + i) % x_sample.shape[0] for i in range(NSAMP)]
        m = dict(shared)
        m["x_p"] = x_prompt[pb]
        m["x_s"] = np.ascontiguousarray(x_sample[ss].reshape(NSAMP * SLEN, D))
        m["c_all"] = np.ascontiguousarray(np.concatenate([c_prompt[pb:pb + 1], c_sample[ss]], axis=0).reshape(3 * DC, 128))
        m["sC_in"] = np.ascontiguousarray(sC_[:, ss])
        m["sn_in"] = np.ascontiguousarray(sn_[:, ss])
        m["sm_in"] = np.ascontiguousarray(sm_[:, ss])
        m["sS_in"] = np.ascontiguousarray(sS_[:, ss])
        in_maps.append(m)
    res = run_bass_kernel_spmd(nc, in_maps, core_ids=list(range(ncores)))
    R = res.results
    nS = x_sample.shape[0]
    y_prompt = np.stack([R[b]["y_p"] for b in range(B)]) if ncores >= B else np.stack([R[0]["y_p"]])
    def gather_s(name, axis_first):
        outs = [None] * nS
        for c in range(ncores):
            for i in range(NSAMP):
                sidx = NSAMP * c + i
                if sidx < nS and outs[sidx] is None:
                    a = R[c][name]
                    outs[sidx] = a[i * SLEN:(i + 1) * SLEN] if axis_first else a[:, i]
        return outs
    ys = gather_s("y_s", True)
    have = [o for o in ys if o is not None]
    y_sample = np.stack(have)
    nb = min(B, ncores)
    pCo = np.stack([R[b]["pC"] for b in range(nb)], axis=1)
    pno = np.stack([R[b]["pn"] for b in range(nb)], axis=1)
    pmo = np.stack([R[b]["pm"] for b in range(nb)], axis=1)
    pSo = np.stack([R[b]["pS"] for b in range(nb)], axis=1)
    def st(name):
        o = [x for x in gather_s(name, False) if x is not None]
        return np.stack(o, axis=1)
    return (y_prompt.astype(np.float32), y_sample.astype(np.float32), pCo, pno, pmo, pSo,
            st("sC"), st("sn"), st("sm"), st("sS"))
```

```python
import numpy as np
import ml_dtypes
import concourse.bass as bass
import concourse.mybir as mybir
from concourse.bass_utils import run_bass_kernel_spmd

F32 = mybir.dt.float32
BF16 = mybir.dt.bfloat16
AF = mybir.ActivationFunctionType
ALU = mybir.AluOpType
AX = mybir.AxisListType

D = 4096
DC = 32
DEPTH = 2
M_HEADS, M_DK, M_DV = 4, 256, 512
H_HEADS, H_DK = 16, 128
D_FF = 16384
IN_COLS = 14344
C_MQ, C_MK, C_MV, C_MO, C_MG, C_HQ, C_HF, C_HI, C_HG = 0, 1024, 2048, 4096, 6144, 6152, 8200, 10248, 12296
EPS = 1e-6
NEG = -1e30
TT = 256
KC = 4
NRING = 6
NSAMP = 2
SLEN = 64


class Prog:
    ENG = ("pe", "act", "dve", "pool", "sp")

    def __init__(self, nc):
        self.nc = nc
        self.streams = {e: [] for e in self.ENG}
        self.res = {}
        self.dma_sems = []
        self.dma_rr = 0
        self.dma_last = {}
        self.named_sems = {}

    @staticmethod
    def key(r):
        if isinstance(r, (str, tuple)):
            return r
        return r.name

    def _deps(self, R, W):
        deps = []
        for r in R:
            st = self.res.setdefault(self.key(r), {"w": None, "r": {}, "rd": []})
            if st["w"] is not None:
                deps.append(st["w"])
        for w in W:
            st = self.res.setdefault(self.key(w), {"w": None, "r": {}, "rd": []})
            if st["w"] is not None:
                deps.append(st["w"])
            deps.extend(st["r"].values())
            deps.extend(st["rd"])
        return deps

    def _commit(self, ev, R, W):
        for r in R:
            st = self.res[self.key(r)]
            if ev[0] == "op":
                st["r"][ev[1]["eng"]] = ev
            else:
                st["rd"].append(ev)
        for w in W:
            st = self.res[self.key(w)]
            st["w"] = ev
            st["r"] = {}
            st["rd"] = []

    @staticmethod
    def _psum_fix(self_key, R, W):
        R2, W2 = [], list(W)
        for r in R:
            k = self_key(r)
            if isinstance(k, str) and k.startswith("ps") and k[2:].isdigit():
                W2.append(r)
            else:
                R2.append(r)
        return R2, W2

    def op(self, eng, fn, R=(), W=()):
        R, W = self._psum_fix(self.key, R, W)
        deps = self._deps(R, W)
        ins = {"eng": eng, "fn": fn, "deps": deps, "kind": "op", "needed": False, "ticket": None}
        self.streams[eng].append(ins)
        self._commit(("op", ins), R, W)
        return ins

    def dma(self, eng, out, in_, R=(), W=(), semkey=None, **kw):
        deps = self._deps(R, W)
        if semkey is None:
            semkey = ("rr", self.dma_rr % 24)
            self.dma_rr += 1
        prev = self.dma_last.get(semkey)
        if prev is not None:
            deps.append(("dma", prev))
        cnt = (prev["count"] if prev else 0) + 16
        ins = {"eng": eng, "out": out, "in_": in_, "deps": deps, "kind": "dma", "semkey": semkey,
               "count": cnt, "kw": kw}
        self.dma_last[semkey] = ins
        self.streams[eng].append(ins)
        self._commit(("dma", ins), R, W)
        return ins

    def emit(self, final_waits):
        nc = self.nc
        for e in self.ENG:
            for ins in self.streams[e]:
                for kind, d in ins["deps"]:
                    if kind == "op":
                        d["needed"] = True
        for e in self.ENG:
            t = 0
            for ins in self.streams[e]:
                if ins["kind"] == "op" and ins["needed"]:
                    t += 1
                    ins["ticket"] = t
        import contextlib
        with contextlib.ExitStack() as es:
            esem = {e: es.enter_context(nc.semaphore("sem_" + e)) for e in self.ENG}
            dsem = {}
            for e in self.ENG:
                for ins in self.streams[e]:
                    if ins["kind"] == "dma" and ins["semkey"] not in dsem:
                        dsem[ins["semkey"]] = es.enter_context(nc.semaphore("dsem%d" % len(dsem)))
            block = es.enter_context(nc.Block())

            def run(engname, E):
                seen = {}
                def wait(sem_id, sem, val):
                    if seen.get(sem_id, 0) >= val:
                        return
                    seen[sem_id] = val
                    E.wait_ge(sem, val)
                for ins in self.streams[engname]:
                    for kind, d in ins["deps"]:
                        if kind == "op":
                            if d["eng"] == engname and engname == "pe":
                                continue
                            wait(d["eng"], esem[d["eng"]], d["ticket"])
                        else:
                            wait(d["semkey"], dsem[d["semkey"]], d["count"])
                    if ins["kind"] == "op":
                        r = ins["fn"](E)
                        if ins["needed"]:
                            r.then_inc(esem[engname], 1)
                    else:
                        E.dma_start(out=ins["out"], in_=ins["in_"], **ins["kw"]).then_inc(dsem[ins["semkey"]], 16)
                if engname == "sp":
                    for kind, d in final_waits:
                        if kind == "op":
                            wait(d["eng"], esem[d["eng"]], d["ticket"])
                        else:
                            wait(d["semkey"], dsem[d["semkey"]], d["count"])

            @block.tensor
            def _(E):
                run("pe", E)

            @block.scalar
            def _(E):
                run("act", E)

            @block.vector
            def _(E):
                run("dve", E)

            @block.gpsimd
            def _(E):
                run("pool", E)

            @block.sync
            def _(E):
                run("sp", E)


def build_program(SEQ):
    nc = bass.Bass("TRN2", target_bir_lowering=False)
    P = Prog(nc)
    import contextlib
    es = contextlib.ExitStack()

    def pbc(ap1d, n):
        return ap1d.rearrange("(o n) -> o n", o=1).broadcast_to([128, n])

    def dram(name, shape, kind="ExternalInput", dt=F32):
        return nc.dram_tensor(name, list(shape), dt, kind=kind).ap()

    def sb(name, shape, dt=F32):
        return es.enter_context(nc.sbuf_tensor(name, list(shape), dt))

    x_p = dram("x_p", [SEQ, D])
    x_s = dram("x_s", [NSAMP * SLEN, D])
    c_all = dram("c_all", [3 * DC, 128])
    sC_in = dram("sC_in", [DEPTH, NSAMP, M_HEADS, M_DK, M_DV])
    sn_in = dram("sn_in", [DEPTH, NSAMP, M_HEADS, M_DK])
    sm_in = dram("sm_in", [DEPTH, NSAMP, M_HEADS])
    sS_in = dram("sS_in", [DEPTH, NSAMP, H_HEADS, H_DK, H_DK])
    w_mod = dram("w_mod", [DEPTH, D, 6 * D])
    b_mod = dram("b_mod", [DEPTH * 192, 128])
    n1g = dram("norm1_g", [DEPTH * DC, 128])
    w_in = dram("w_in", [DEPTH, D, IN_COLS])
    b_gate = dram("b_gate", [DEPTH, 8])
    lowb = dram("lower_bounds", [DEPTH * 16, 128])
    mng = dram("mlstm_norm_g", [DEPTH, 2048])
    hng = dram("hgrn_norm_g", [DEPTH * 16, 128])
    w_out = dram("w_out", [DEPTH, D, D])
    n2g = dram("norm2_g", [DEPTH * DC, 128])
    w_up = dram("w_up", [DEPTH, D, D_FF])
    w_down = dram("w_down", [DEPTH, D_FF, D])
    fing = dram("final_g", [DC, 128])
    k_f32 = dram("k_f32", [128, 4, 128])
    k_bf = dram("k_bf", [128, 4, 128], dt=BF16)
    k_scan = dram("k_scan", [128, TT])

    y_p = dram("y_p", [SEQ, D], kind="ExternalOutput")
    y_s = dram("y_s", [NSAMP * SLEN, D], kind="ExternalOutput")
    pC = dram("pC", [DEPTH, M_HEADS, M_DK, M_DV], kind="ExternalOutput")
    pn = dram("pn", [DEPTH, M_HEADS, M_DK], kind="ExternalOutput")
    pm = dram("pm", [DEPTH, M_HEADS], kind="ExternalOutput")
    pS = dram("pS", [DEPTH, H_HEADS, H_DK, H_DK], kind="ExternalOutput")
    sC = dram("sC", [DEPTH, NSAMP, M_HEADS, M_DK, M_DV], kind="ExternalOutput")
    sn = dram("sn", [DEPTH, NSAMP, M_HEADS, M_DK], kind="ExternalOutput")
    sm = dram("sm", [DEPTH, NSAMP, M_HEADS], kind="ExternalOutput")
    sS = dram("sS", [DEPTH, NSAMP, H_HEADS, H_DK, H_DK], kind="ExternalOutput")

    xT = sb("xT", [128, DC, TT])
    bufA = sb("bufA", [128, DC, TT], BF16)
    bufB = sb("bufB", [128, DC, TT], BF16)
    ring = [sb("ring%d" % i, [128, KC, 512], BF16) for i in range(NRING)]
    kf = sb("kf", [128, 4, 128])
    kb = sb("kb", [128, 4, 128], BF16)
    kscan = sb("kscan", [128, TT])
    cst = sb("cst", [128, 4])
    modv = [sb("modv%d" % l, [128, 192, 3]) for l in range(DEPTH)]
    A1 = [sb("A1_%d" % l, [128, DC, 3]) for l in range(DEPTH)]
    A2 = [sb("A2_%d" % l, [128, DC, 3]) for l in range(DEPTH)]
    vecs = sb("vecs", [128, 2 * DC + 2 * DC + DC + 2 * 16 + 2 * 16 + 192 * 2 + 96])
    V_N1, V_N2, V_FIN, V_LB, V_HG, V_BM, V_C = 0, 64, 128, 160, 192, 224, 224 + 384
    vstage = sb("vstage", [128, 128])
    cT = sb("cT", [128, DC, 3], BF16)
    lbv = sb("lbv", [128, 2, 16, 3])
    rstd = sb("rstd", [128, TT])
    tmpn = sb("tmpn", [128, TT])
    bg_row = sb("bg_row", [128, DEPTH, 8])
    qT = [sb("qT%d" % i, [128, 2, TT], BF16) for i in range(2)]
    kT = [sb("kT%d" % i, [128, 2, TT], BF16) for i in range(2)]
    k_tok = [sb("k_tok%d" % i, [128, 2, 256], BF16) for i in range(2)]
    v_tok = [sb("v_tok%d" % i, [128, 2, 512], BF16) for i in range(2)]
    og = [sb("og%d" % i, [128, 2, 512], BF16) for i in range(2)]
    mg_row = [sb("mg_row%d" % i, [128, 512]) for i in range(2)]
    gates = sb("gates", [128, 2, 8])
    lf = sb("lf", [128, 2, 4])
    Cst = sb("Cst", [128, 2, 512])
    Cbf = sb("Cbf", [128, 2, 512], BF16)
    nst = sb("nst", [128, 2])
    nbf = sb("nbf", [128, 2], BF16)
    mst = sb("mst", [128, 1])
    sm_small = sb("sm_small", [128, 24])
    Dm = sb("Dm", [128, 128])
    Em = sb("Em", [128, 128])
    Sm = sb("Sm", [128, 128], BF16)
    SmT = sb("SmT", [128, 128], BF16)
    diagu = sb("diagu", [128, 128])
    hsum = sb("hsum", [128, 512])
    numt = sb("numt", [128, 512])
    junk = sb("junk", [128, 512])
    otok = sb("otok", [128, 512], BF16)
    ks = sb("ks", [128, 256], BF16)
    HH = 4
    hqT = [sb("hqT%d" % i, [128, HH, TT], BF16) for i in range(2)]
    sgT = [sb("sgT%d" % i, [128, HH, TT]) for i in range(2)]
    hgT = [sb("hgT%d" % i, [128, HH, TT], BF16) for i in range(2)]
    hv_tok = [sb("hv_tok%d" % i, [128, 2, HH * 128], BF16) for i in range(2)]
    Sst = [sb("Sst%d" % h, [128, 128]) for h in range(HH)]
    Sbf = [sb("Sbf%d" % h, [128, 128], BF16) for h in range(HH)]
    fT = sb("fT", [128, TT])
    gT = sb("gT", [128, TT])
    GT = sb("GT", [128, TT])
    kkT = sb("kkT", [128, TT])
    eT = sb("eT", [128, TT])
    kx32 = sb("kx32", [128, TT])
    qxT = [sb("qxT%d" % i, [128, TT], BF16) for i in range(HH)]
    kxT = [sb("kxT%d" % i, [128, TT], BF16) for i in range(HH)]
    kdT = [sb("kdT%d" % i, [128, TT], BF16) for i in range(HH)]
    eGl = [sb("eGl%d" % i, [128, TT // 16]) for i in range(HH)]
    kd_tok = sb("kd_tok", [128, 128], BF16)
    kdm = [sb("kdm%d" % i, [128, 8, 128], BF16) for i in range(HH)]
    Am = sb("Am", [128, 128], BF16)
    oT = sb("oT", [128, 128])
    osq = sb("osq", [128, 128], BF16)
    orst = sb("orst", [128, 128])
    uT = [sb("uT%d" % i, [128, 8, TT], BF16) for i in range(2)]


    ps = [es.enter_context(nc.psum_tensor("ps%d" % i, [128, 512], F32)) for i in range(8)]

    IDF, TRIU, MNEG, ONESF = kf[:, 0, :], kf[:, 1, :], kf[:, 2, :], kf[:, 3, :]
    IDB, ONESB, HMASK, RMASK = kb[:, 0, :], kb[:, 1, :], kb[:, 2, :], kb[:, 3, :]
    CEPS, CONE, CZERO = cst[:, 0:1], cst[:, 1:2], cst[:, 2:3]

    def act(out, in_, func, R, W, bias=None, scale=1.0):
        kw = {}
        if bias is not None:
            kw["bias"] = bias
        return P.op("act", lambda E: E.activation(out=out, in_=in_, func=func, scale=scale, **kw), R, W)

    def ts(out, in0, s1, s2, op0, op1, R, W, eng="dve"):
        return P.op(eng, lambda E: E.tensor_scalar(out=out, in0=in0, scalar1=s1, scalar2=s2, op0=op0, op1=op1), R, W)

    def stt(out, in0, s, in1, op0, op1, R, W):
        return P.op("dve", lambda E: E.scalar_tensor_tensor(out=out, in0=in0, scalar=s, in1=in1, op0=op0, op1=op1), R, W)

    def tt(out, in0, in1, op, R, W, eng="dve"):
        return P.op(eng, lambda E: E.tensor_tensor(out=out, in0=in0, in1=in1, op=op), R, W)

    def red(out, in_, op, R, W):
        return P.op("dve", lambda E: E.tensor_reduce(out=out, in_=in_, axis=AX.X, op=op), R, W)

    def recip(out, in_, R, W):
        return P.op("dve", lambda E: E.reciprocal(out=out, in_=in_), R, W)

    def cp(out, in_, R, W, eng="dve"):
        if eng == "act":
            return P.op("act", lambda E: E.copy(out=out, in_=in_), R, W)
        return P.op(eng, lambda E: E.tensor_copy(out=out, in_=in_), R, W)

    def mm(out, lhsT, rhs, start, stop, R, W):
        return P.op("pe", lambda E: E.matmul(out, lhsT, rhs, start=start, stop=stop), R, W)

    def tr(out, in_, ident, R, W):
        return P.op("pe", lambda E: E.transpose(out, in_, ident), R, W)

    def memset(ap, v, W, eng="dve"):
        return P.op(eng, lambda E: E.memset(ap, v), (), W)

    P.dma("sp", kf[:], k_f32, W=[kf])
    P.dma("sp", kb[:], k_bf, W=[kb])
    P.dma("sp", kscan[:], k_scan, W=[kscan])
    memset(cst[:, 0:1], EPS, [cst])
    memset(cst[:, 1:2], 1.0, [cst])
    memset(cst[:, 2:3], 0.0, [cst])

    def load_vec_fm(src_rows, nrows, dst):
        P.dma("sp", vstage[:nrows, :], src_rows, W=[vstage])
        tr(ps[0][:, :nrows], vstage[:nrows, :], IDF[:nrows, :nrows], [vstage, kf], [ps[0]])
        cp(dst, ps[0][:, :nrows], [ps[0]], [vecs])

    load_vec_fm(n1g, 64, vecs[:, V_N1:V_N1 + 64])
    load_vec_fm(n2g, 64, vecs[:, V_N2:V_N2 + 64])
    load_vec_fm(fing, 32, vecs[:, V_FIN:V_FIN + 32])
    load_vec_fm(lowb, 32, vecs[:, V_LB:V_LB + 32])
    load_vec_fm(hng, 32, vecs[:, V_HG:V_HG + 32])
    for i in range(3):
        load_vec_fm(b_mod[i * 128:(i + 1) * 128, :], 128, vecs[:, V_BM + i * 128:V_BM + (i + 1) * 128])
    load_vec_fm(c_all, 96, vecs[:, V_C:V_C + 96])
    act(cT[:].rearrange("p c s -> p s c"), vecs[:, V_C:V_C + 96].rearrange("p (s c) -> p s c", s=3), AF.Silu, [vecs], [cT])
    l0 = vecs[:, V_LB:V_LB + 16]
    l1 = vecs[:, V_LB + 16:V_LB + 32]
    tt(tmpn[:, 0:16], l1, l0, ALU.subtract, [vecs], [tmpn])
    act(lbv[:, 1, :, 0], tmpn[:, 0:16], AF.Sigmoid, [tmpn], [lbv])
    act(tmpn[:, 16:32], tmpn[:, 0:16], AF.Sigmoid, [tmpn], [tmpn], scale=-1.0)
    tt(lbv[:, 0, :, 0], tmpn[:, 16:32], tmpn[:, 16:32], ALU.subtract, [tmpn], [lbv])
    for l in range(DEPTH):
        ts(lbv[:, l, :, 1], lbv[:, l, :, 0], -1.0, 1.0, ALU.mult, ALU.add, [lbv], [lbv])
        ts(lbv[:, l, :, 2], lbv[:, l, :, 0], 1.0, -1.0, ALU.mult, ALU.add, [lbv], [lbv])
    P.dma("sp", bg_row[:].rearrange("p l g -> p (l g)"),
          b_gate.rearrange("(o l) g -> o (l g)", o=1).broadcast_to([128, DEPTH * 8]), W=[bg_row])

    ring_state = {"i": 0}

    def load_w(Wl, kc0, nk, col0, ncols):
        i = ring_state["i"]
        ring_state["i"] += 1
        slot = i % NRING
        src = Wl.rearrange("(kc p) c -> p kc c", p=128)[:, kc0:kc0 + nk, col0:col0 + ncols]
        P.dma("pool", ring[slot][:, :nk, :ncols], src, W=[ring[slot]], semkey=("ring", slot))
        return ring[slot]

    bank_rr = {"i": 0}

    def proj_g(Wl, nkc, col0, ncols, mode, actT, actR, blocks, evac, base=None):
        if base is None:
            base = 4 * (bank_rr["i"] % 2)
            bank_rr["i"] += 1
        if mode == "FM":
            nout = (ncols + 127) // 128
        else:
            nout = len(blocks)
        assert nout <= 4
        for kg in range(0, nkc, KC):
            nk = min(KC, nkc - kg)
            slot = load_w(Wl, kg, nk, col0, ncols)
            for kk in range(nk):
                kc = kg + kk
                a = actT(kc)
                for j in range(nout):
                    if mode == "FM":
                        cw = min(128, ncols - j * 128)
                        ntok = a.shape[-1]
                        mm(ps[base + j][:cw, :ntok], slot[:, kk, j * 128:j * 128 + cw], a,
                           kc == 0, kc == nkc - 1, [slot] + actR(kc), [ps[base + j]])
                    else:
                        c0, n = blocks[j]
                        mm(ps[base + j][:n, :ncols], a[:, c0:c0 + n], slot[:, kk, :ncols],
                           kc == 0, kc == nkc - 1, [slot] + actR(kc), [ps[base + j]])
            yield
        for j in range(nout):
            if mode == "FM":
                cw = min(128, ncols - j * 128)
                evac(j, ps[base + j][:cw, :], ps[base + j])
            else:
                c0, n = blocks[j]
                evac(j, ps[base + j][:n, :ncols], ps[base + j])
        yield

    def drain(g):
        for _ in g:
            pass

    def proj(*a, **k):
        drain(proj_g(*a, **k))

    def par(ga, gb, ra=1, rb=1):
        alive_a = alive_b = True
        while alive_a or alive_b:
            for _ in range(ra):
                if alive_a:
                    try:
                        next(ga)
                    except StopIteration:
                        alive_a = False
            for _ in range(rb):
                if alive_b:
                    try:
                        next(gb)
                    except StopIteration:
                        alive_b = False


    for l in range(DEPTH):
        for g in range(48):
            def ev(j, pap, pres, l=l, g=g):
                ch = g * 4 + j
                bm = vecs[:, V_BM + l * 192 + ch:V_BM + l * 192 + ch + 1]
                ts(modv[l][:, ch, :], pap[:, :3], bm, None, ALU.add, ALU.bypass, [pres, vecs], [modv[l]])
            proj(w_mod[l], DC, g * 512, 512, "FM", lambda kc: cT[:, kc, :], lambda kc: [cT], None, ev)
        for (Ax, voff, sck) in ((A1, V_N1, 1), (A2, V_N2, 4)):
            for s in range(3):
                stt(Ax[l][:, :, s], modv[l][:, sck * 32:(sck + 1) * 32, s], 1.0, vecs[:, voff + l * 32:voff + (l + 1) * 32],
                    ALU.add, ALU.mult, [modv[l], vecs], [Ax[l]])

    def modk(l, kind, c, s):
        return modv[l][:, kind * 32 + c, s:s + 1]

    def adaln(l, which, ntok, segs):
        act(bufA[:, :, :ntok], xT[:, :, :ntok], AF.Square, [xT], [bufA])
        for c in range(DC):
            mm(ps[0][:, :ntok], ONESB, bufA[:, c, :ntok], c == 0, c == DC - 1, [bufA, kb], [ps[0]])
        act(tmpn[:, :ntok], ps[0][:, :ntok], AF.Sqrt, [ps[0], cst], [tmpn], bias=CEPS, scale=1.0 / D)
        recip(rstd[:, :ntok], tmpn[:, :ntok], [tmpn], [rstd])
        if which == 0:
            Ax, shk = A1, 0
        elif which == 1:
            Ax, shk = A2, 3
        for c in range(DC):
            for (c0, n, s) in segs:
                stt(tmpn[:, c0:c0 + n], xT[:, c, c0:c0 + n], Ax[l][:, c, s:s + 1], rstd[:, c0:c0 + n],
                    ALU.mult, ALU.mult, [xT, Ax[l], rstd], [tmpn])
                act(bufA[:, c, c0:c0 + n], tmpn[:, c0:c0 + n], AF.Identity, [tmpn, modv[l]], [("bufA", c)],
                    bias=modk(l, shk, c, s))
        return

    def bufA_R(kc):
        return [bufA, ("bufA", kc)]

    out_dmas = []
    RB = 4

    def mlstm_chunk_g(l, h, c0, n, bi):
        zb = h % 2
        q_, k_, kt_, vt_, og_, mgr = qT[zb], kT[zb], k_tok[zb], v_tok[zb], og[zb], mg_row[zb]
        sml = sm_small
        col = lambda i: sml[:n, i:i + 1]
        colP = lambda i: sml[:, i:i + 1]
        ig = gates[:n, bi, h:h + 1]
        lfv = lf[:n, bi, h:h + 1]
        p0, p1, p2, p3 = ps[0], ps[1], ps[2], ps[3]
        pst = p1[:].bitcast(BF16)
        mm(p0[:n, 200:201], TRIU[:n, :n], lfv, True, True, [kf, lf], [p0])
        mm(p0[:, 201:202], ONESF[:n, :], lfv, True, True, [kf, lf], [p0])
        for c in range(2):
            mm(p0[:n, :n], q_[:, c, c0:c0 + n], k_[:, c, c0:c0 + n], c == 0, c == 1, [q_, k_], [p0])
        for c in range(2):
            mm(p2[:n, :], q_[:, c, c0:c0 + n], Cbf[:, c, :], c == 0, c == 1, [q_, Cbf], [p2])
        for c in range(2):
            mm(p0[:n, 202:203], q_[:, c, c0:c0 + n], nbf[:, c:c + 1], c == 0, c == 1, [q_, nbf], [p0])
        yield
        cp(col(0), p0[:n, 200:201], [p0], [sml])
        cp(colP(1), p0[:, 201:202], [p0], [sml])
        tt(col(2), ig, col(0), ALU.subtract, [gates, sml], [sml])
        ts(diagu[:n, :n], IDF[:n, :n], col(2), None, ALU.mult, ALU.bypass, [kf, sml], [diagu])
        cp(col(21), p0[:n, 202:203], [p0], [sml])
        cp(Dm[:n, :n], p0[:n, :n], [p0], [Dm], eng="act")
        mm(p0[:, 256:256 + n], ONESF[:n, :], diagu[:n, :n], True, True, [kf, diagu], [p0])
        yield
        stt(Em[:n, :n], p0[:n, 256:256 + n], col(0), MNEG[:n, :n], ALU.add, ALU.add, [p0, sml, kf], [Em])
        red(col(3), Em[:n, :n], ALU.max, [Em], [sml])
        red(colP(4), p0[:, 256:256 + n], ALU.max, [p0], [sml])
        tt(col(5), col(0), mst[:n, :], ALU.add, [sml, mst], [sml])
        tt(col(6), col(5), col(3), ALU.max, [sml], [sml])
        ts(col(7), col(6), -1.0, None, ALU.mult, ALU.bypass, [sml], [sml])
        act(col(8), col(5), AF.Exp, [sml], [sml], bias=col(7))
        act(Em[:n, :n], Em[:n, :n], AF.Exp, [Em, sml], [Em], bias=col(7))
        tt(Sm[:n, :n], Dm[:n, :n], Em[:n, :n], ALU.mult, [Dm, Em], [Sm])
        red(col(9), Sm[:n, :n], ALU.add, [Sm], [sml])
        tr(pst[:n, :n], Sm[:n, :n], IDB[:n, :n], [Sm, kb], [p1])
        yield
        cp(SmT[:n, :n], pst[:n, :n], [p1], [SmT], eng="act")
        P.op("act", lambda E: E.activation(out=hsum[:n, :], in_=p2[:n, :], func=AF.Copy, scale=col(8)), [p2, sml], [hsum])
        mm(p3[:n, :], SmT[:n, :n], vt_[:n, bi, :], True, True, [SmT, vt_], [p3])
        yield
        tt(numt[:n, :], p3[:n, :], hsum[:n, :], ALU.add, [p3, hsum], [numt])
        stt(col(10), col(21), col(8), col(9), ALU.mult, ALU.add, [sml], [sml])
        ts(col(20), col(10), -1.0, None, ALU.mult, ALU.bypass, [sml], [sml])
        tt(col(10), col(10), col(20), ALU.max, [sml], [sml])
        act(col(11), col(6), AF.Exp, [sml], [sml], scale=-1.0)
        tt(col(10), col(10), col(11), ALU.max, [sml], [sml])
        recip(col(12), col(10), [sml], [sml])
        act(junk[:n, :], numt[:n, :], AF.Square, [numt], [junk])
        red(col(13), junk[:n, :], ALU.add, [junk], [sml])
        tt(col(14), col(12), col(12), ALU.mult, [sml], [sml])
        tt(col(14), col(14), col(13), ALU.mult, [sml], [sml])
        act(col(15), col(14), AF.Sqrt, [sml, cst], [sml], bias=CEPS[:n, :], scale=1.0 / M_DV)
        recip(col(15), col(15), [sml], [sml])
        tt(col(15), col(15), col(12), ALU.mult, [sml], [sml])
        tt(junk[:n, :], og_[:n, bi, :], mgr[:n, :], ALU.mult, [og_, mgr, junk], [junk])
        stt(otok[:n, :], numt[:n, :], col(15), junk[:n, :], ALU.mult, ALU.mult, [numt, sml, junk], [otok])
        tt(colP(16), mst[:, :], colP(4), ALU.max, [mst, sml], [sml])
        ts(colP(17), colP(16), -1.0, None, ALU.mult, ALU.bypass, [sml], [sml])
        act(colP(18), mst[:, :], AF.Exp, [mst, sml], [sml], bias=colP(17))
        act(col(19), col(2), AF.Exp, [sml], [sml], bias=col(17))
        tt(mst[:, :], colP(16), colP(1), ALU.add, [sml], [mst])
        ts(ks[:n, :], kt_[:n, bi, :], col(19), None, ALU.mult, ALU.bypass, [kt_, sml], [ks])
        for j in range(4):
            tr(pst[:, 512 + j * 128:512 + j * 128 + n], otok[:n, j * 128:(j + 1) * 128], IDB[:n, :n], [otok, kb], [p1])
        for c in range(2):
            pc = (p2, p3)[c]
            mm(pc[:, :], ks[:n, c * 128:(c + 1) * 128], vt_[:n, bi, :], True, True, [ks, vt_], [pc])
            mm(p0[:, 204 + c:205 + c], ks[:n, c * 128:(c + 1) * 128], ONESB[:n, 0:1], True, True, [ks, kb], [p0])
        yield
        cp(bufB[:, h * 4:(h + 1) * 4, c0:c0 + n], pst[:, 512:1024].rearrange("p (j t) -> p j t", j=4)[:, :, :n],
           [p1], [bufB], eng="act")
        for c in range(2):
            pc = (p2, p3)[c]
            stt(Cst[:, c, :], Cst[:, c, :], colP(18), pc[:, :], ALU.mult, ALU.add, [Cst, sml, pc], [Cst])
            cp(Cbf[:, c, :], Cst[:, c, :], [Cst], [Cbf], eng="act")
        stt(nst[:, :], nst[:, :], colP(18), p0[:, 204:206], ALU.mult, ALU.add, [nst, sml, p0], [nst])
        cp(nbf[:, :], nst[:, :], [nst], [nbf])
        yield

    def mlstm_state_load(l, h, run):
        if run["zero"]:
            memset(Cst[:], 0.0, [Cst])
            memset(Cbf[:], 0.0, [Cbf])
            memset(nst[:], 0.0, [nst])
            memset(nbf[:], 0.0, [nbf])
            memset(mst[:], 0.0, [mst])
            return
        Csrc, nsrc, msrc = run["C"], run["n"], run["m"]
        P.dma("sp", Cst[:], Csrc[l, h].rearrange("(c p) e -> p c e", p=128), R=[(run["key"], l, h, "C")], W=[Cst])
        P.dma("sp", nst[:], nsrc[l, h].rearrange("(c p) -> p c", p=128), R=[(run["key"], l, h, "n")], W=[nst],
              allow_slow_non_contiguous=True)
        P.dma("sp", mst[:], pbc(msrc[l, h:h + 1], 1), R=[(run["key"], l, h, "m")], W=[mst])
        cp(Cbf[:], Cst[:], [Cst], [Cbf], eng="act")
        cp(nbf[:], nst[:], [nst], [nbf])

    def mlstm_state_store(l, h, run):
        Cd, nd, md = run["Co"], run["no"], run["mo"]
        out_dmas.append(P.dma("sp", Cd[l, h].rearrange("(c p) e -> p c e", p=128), Cst[:], R=[Cst], W=[(run["okey"], l, h, "C")]))
        out_dmas.append(P.dma("sp", nd[l, h].rearrange("(c p) -> p c", p=128), nst[:], R=[nst], W=[(run["okey"], l, h, "n")],
                              allow_slow_non_contiguous=True))
        out_dmas.append(P.dma("sp", md[l, h:h + 1].rearrange("(o n) -> o n", o=1), mst[0:1, :], R=[mst], W=[(run["okey"], l, h, "m")]))

    def gates_proj(l, tile):
        ntok, blocks = tile["ntok"], tile["blocks"]
        hT = lambda kc: bufA[:, kc, :ntok]
        def ev_g(b, pap, pres):
            c0, n = blocks[b]
            tt(gates[:n, b, :], pap, bg_row[:n, l, :], ALU.add, [pres, bg_row], [gates])
            act(lf[:n, b, :], gates[:n, b, 4:8], AF.Exp, [gates], [lf], scale=-1.0)
            act(lf[:n, b, :], lf[:n, b, :], AF.Ln, [lf, cst], [lf], bias=CONE[:n, :])
            ts(lf[:n, b, :], lf[:n, b, :], -1.0, None, ALU.mult, ALU.bypass, [lf], [lf])
        proj(w_in[l], DC, C_MG, 8, "TM", hT, bufA_R, blocks, ev_g, base=RB)

    def proj_m_g(l, h, tile):
        ntok, blocks = tile["ntok"], tile["blocks"]
        zb = h % 2
        Wl = w_in[l]
        hT = lambda kc: bufA[:, kc, :ntok]
        P.dma("sp", mg_row[zb][:], pbc(mng[l, h * 512:(h + 1) * 512], 512), W=[mg_row[zb]])
        def ev_q(j, pap, pres):
            ts(qT[zb][:, j, :ntok], pap[:, :ntok], M_DK ** -0.5, None, ALU.mult, ALU.bypass, [pres], [qT[zb]])
        def ev_k(j, pap, pres):
            cp(kT[zb][:, j, :ntok], pap[:, :ntok], [pres], [kT[zb]], eng="act")
        def ev_kt(b, pap, pres):
            cp(k_tok[zb][:blocks[b][1], b, :], pap, [pres], [k_tok[zb]])
        def ev_v(b, pap, pres):
            cp(v_tok[zb][:blocks[b][1], b, :], pap, [pres], [v_tok[zb]], eng="act")
        def ev_o(b, pap, pres):
            act(og[zb][:blocks[b][1], b, :], pap, AF.Sigmoid, [pres], [og[zb]])
        yield from proj_g(Wl, DC, C_MQ + h * 256, 256, "FM", hT, bufA_R, None, ev_q, base=RB)
        yield from proj_g(Wl, DC, C_MK + h * 256, 256, "FM", hT, bufA_R, None, ev_k, base=RB)
        yield from proj_g(Wl, DC, C_MK + h * 256, 256, "TM", hT, bufA_R, blocks, ev_kt, base=RB)
        yield from proj_g(Wl, DC, C_MV + h * 512, 512, "TM", hT, bufA_R, blocks, ev_v, base=RB)
        yield from proj_g(Wl, DC, C_MO + h * 512, 512, "TM", hT, bufA_R, blocks, ev_o, base=RB)

    def rec_m_g(l, h, tile):
        blocks = tile["blocks"]
        for run in tile["runs"]:
            mlstm_state_load(l, h, run)
            yield
            for bi in run["blocks"]:
                c0, n = blocks[bi]
                yield from mlstm_chunk_g(l, h, c0, n, bi)
            mlstm_state_store(l, h, run)
            yield

    RK = [ps[1 + i // 2] for i in range(HH)]
    UK = [ps[3] for i in range(HH)]

    def hbarrier():
        return

    def proj_h_g(l, q, tile):
        ntok, blocks = tile["ntok"], tile["blocks"]
        zb = q % 2
        Wl = w_in[l]
        hT = lambda kc: bufA[:, kc, :ntok]
        hb = q * HH
        def ev_q(j, pap, pres):
            act(hqT[zb][:, j, :ntok], pap[:, :ntok], AF.Silu, [pres], [hqT[zb]])
        def ev_f(j, pap, pres):
            act(sgT[zb][:, j, :ntok], pap[:, :ntok], AF.Sigmoid, [pres], [sgT[zb]])
        def ev_g(j, pap, pres):
            act(hgT[zb][:, j, :ntok], pap[:, :ntok], AF.Silu, [pres], [hgT[zb]])
        def ev_v(b, pap, pres):
            cp(hv_tok[zb][:blocks[b][1], b, :], pap, [pres], [hv_tok[zb]])
        yield from proj_g(Wl, DC, C_HQ + hb * 128, 512, "FM", hT, bufA_R, None, ev_q, base=RB)
        yield from proj_g(Wl, DC, C_HF + hb * 128, 512, "FM", hT, bufA_R, None, ev_f, base=RB)
        yield from proj_g(Wl, DC, C_HI + hb * 128, 512, "TM", hT, bufA_R, blocks, ev_v, base=RB)
        yield from proj_g(Wl, DC, C_HG + hb * 128, 512, "FM", hT, bufA_R, None, ev_g, base=RB)

    def rec_h_g(l, q, tile):
        ntok, blocks, runs = tile["ntok"], tile["blocks"], tile["runs"]
        zb = q % 2
        nb = ntok // 16
        hq_, sg_, hg_, hv_ = hqT[zb], sgT[zb], hgT[zb], hv_tok[zb]
        pA = ps[0]
        pAb = pA[:].bitcast(BF16)
        def Rap(i):
            return ps[1 + i // 2][:, (i % 2) * 256:(i % 2) * 256 + 256]
        def Uap(i):
            return ps[3][:, i * 128:(i + 1) * 128]
        hbarrier()
        for i in range(HH):
            h = q * HH + i
            lb, oml, moml = lbv[:, l, h, 0:1], lbv[:, l, h, 1:2], lbv[:, l, h, 2:3]
            sg = sg_[:, i, :ntok]
            ts(fT[:, :ntok], sg, oml, lb, ALU.mult, ALU.add, [sg_, lbv], [fT])
            act(gT[:, :ntok], fT[:, :ntok], AF.Ln, [fT], [gT])
            P.op("dve", lambda E: E.tensor_tensor_scan(out=GT[:, :ntok], data0=kscan[:, :ntok], data1=gT[:, :ntok],
                                                       initial=0.0, op0=ALU.mult, op1=ALU.add), [kscan, gT], [GT])
            ts(kkT[:, :ntok], sg, moml, oml, ALU.mult, ALU.add, [sg_, lbv], [kkT])
            act(eT[:, :ntok], GT[:, :ntok], AF.Exp, [GT], [eT])
            tt(qxT[i][:, :ntok], hq_[:, i, :ntok], eT[:, :ntok], ALU.mult, [hq_, eT], [qxT[i]])
            act(eT[:, :ntok], GT[:, :ntok], AF.Exp, [GT], [eT], scale=-1.0)
            tt(kx32[:, :ntok], kkT[:, :ntok], eT[:, :ntok], ALU.mult, [kkT, eT], [kx32])
            cp(kxT[i][:, :ntok], kx32[:, :ntok], [kx32], [kxT[i]], eng="act")
            GTv = GT[:, :ntok].rearrange("p (b k) -> p b k", k=16)
            act(eGl[i][:, :nb], GTv[:, :, 15], AF.Exp, [GT], [eGl[i]])
            tt(kdT[i][:, :ntok].rearrange("p (b k) -> p b k", k=16), kx32[:, :ntok].rearrange("p (b k) -> p b k", k=16),
               eGl[i][:, :nb].unsqueeze(2).broadcast_to([128, nb, 16]), ALU.mult, [kx32, eGl[i]], [kdT[i]])
            yield
        for run in runs:
            for i in range(HH):
                h = q * HH + i
                if run["zero"]:
                    memset(Sst[i][:], 0.0, [Sst[i]])
                    memset(Sbf[i][:], 0.0, [Sbf[i]])
                else:
                    P.dma("sp", Sst[i][:], run["S"][l, h], R=[(run["key"], l, h, "S")], W=[Sst[i]])
                    cp(Sbf[i][:], Sst[i][:], [Sst[i]], [Sbf[i]], eng="act")
            yield
            for bi in run["blocks"]:
                c0, n = blocks[bi]
                nsb = n // 16
                for i in range(HH):
                    vh = hv_[:n, bi, i * 128:(i + 1) * 128]
                    tr(pAb[:n, 512:640], kdT[i][:, c0:c0 + n], IDB, [kdT[i], kb], [pA])
                    mm(pA[:n, :n], kxT[i][:, c0:c0 + n], qxT[i][:, c0:c0 + n], True, True, [kxT[i], qxT[i]], [pA])
                    yield
                    cp(kd_tok[:n, :], pAb[:n, 512:640], [pA], [kd_tok], eng="act")
                    tt(kdm[i][:n, :nsb, :], kd_tok[:n, :].unsqueeze(1).broadcast_to([n, nsb, 128]),
                       RMASK[:n, :nsb].unsqueeze(2).broadcast_to([n, nsb, 128]), ALU.mult, [kd_tok, kb], [kdm[i]])
                    tt(Am[:n, :n], pA[:n, :n], HMASK[:n, :n], ALU.mult, [pA, kb], [Am])
                    mm(Rap(i)[:, 0:n], vh, Am[:n, :n], True, True, [hv_, Am], [RK[i]])
                    yield
                for sbk in range(nsb):
                    t0 = c0 + sbk * 16
                    gi = (c0 // 16) + sbk
                    for i in range(HH):
                        vh = hv_[:n, bi, i * 128:(i + 1) * 128]
                        mm(Rap(i)[:, 128 + sbk * 16:128 + sbk * 16 + 16], Sbf[i][:, :], qxT[i][:, t0:t0 + 16], True, True,
                           [Sbf[i], qxT[i]], [RK[i]])
                        mm(Uap(i), kdm[i][:n, sbk, :], vh, True, True, [kdm[i], hv_], [UK[i]])
                    for i in range(HH):
                        stt(Sst[i][:, :], Sst[i][:, :], eGl[i][:, gi:gi + 1], Uap(i), ALU.mult, ALU.add,
                            [Sst[i], eGl[i], UK[i]], [Sst[i]])
                        cp(Sbf[i][:, :], Sst[i][:, :], [Sst[i]], [Sbf[i]], eng="act")
                    yield
                for i in range(HH):
                    h = q * HH + i
                    cp(oT[:, :n], Rap(i)[:, 0:n], [RK[i]], [oT], eng="act")
                    tt(oT[:, :n], oT[:, :n], Rap(i)[:, 128:128 + n], ALU.add, [oT, RK[i]], [oT])
                    act(osq[:, :n], oT[:, :n], AF.Square, [oT], [osq])
                    mm(pA[:, 384:384 + n], ONESB, osq[:, :n], True, True, [kb, osq], [pA])
                    yield
                    act(orst[:, :n], pA[:, 384:384 + n], AF.Sqrt, [pA, cst], [orst], bias=CEPS, scale=1.0 / H_DK)
                    recip(orst[:, :n], orst[:, :n], [orst], [orst])
                    stt(oT[:, :n], oT[:, :n], vecs[:, V_HG + l * 16 + h:V_HG + l * 16 + h + 1], orst[:, :n], ALU.mult, ALU.mult,
                        [oT, vecs, orst], [oT])
                    tt(bufB[:, 16 + h, c0:c0 + n], oT[:, :n], hg_[:, i, c0:c0 + n], ALU.mult, [oT, hg_], [bufB])
            for i in range(HH):
                h = q * HH + i
                out_dmas.append(P.dma("sp", run["So"][l, h], Sst[i][:], R=[Sst[i]], W=[(run["okey"], l, h, "S")]))
            yield
        hbarrier()
        yield

    def out_proj_g(l, tile, kc0, nkc, base=None):
        ntok, segs = tile["ntok"], tile["segs"]
        Wl = w_out[l][kc0 * 128:(kc0 + nkc) * 128, :]
        for g in range(8):
            def ev(j, pap, pres, g=g):
                c = g * 4 + j
                for (c0, n, s) in segs:
                    stt(xT[:, c, c0:c0 + n], pap[:, c0:c0 + n], modk(l, 2, c, s), xT[:, c, c0:c0 + n], ALU.mult, ALU.add,
                        [pres, modv[l], xT], [xT])
            yield from proj_g(Wl, nkc, g * 512, 512, "FM", lambda kc: bufB[:, kc0 + kc, :ntok], lambda kc: [bufB], None, ev,
                              base=base)

    def mixer(l, tile):
        gates_proj(l, tile)
        pm = [proj_m_g(l, h, tile) for h in range(M_HEADS)]
        rm = [rec_m_g(l, h, tile) for h in range(M_HEADS)]
        NQ = H_HEADS // HH
        ph = [proj_h_g(l, q, tile) for q in range(NQ)]
        rh = [rec_h_g(l, q, tile) for q in range(NQ)]
        drain(pm[0])
        for h in range(M_HEADS - 1):
            par(rm[h], pm[h + 1], 1, 3)
        par(rm[M_HEADS - 1], ph[0], 1, 3)
        for q in range(NQ - 1):
            par(rh[q], ph[q + 1], 1, 1)
        par(rh[NQ - 1], out_proj_g(l, tile, 0, 28, base=RB), 1, 1)
        drain(out_proj_g(l, tile, 28, 4))

    def ffn(l, tile):
        ntok, segs = tile["ntok"], tile["segs"]
        HB = 1024
        for hbk in range(D_FF // HB):
            u = uT[hbk % 2]
            for half in range(2):
                def ev_u(j, pap, pres, u=u, half=half):
                    act(junk[:, :ntok], pap[:, :ntok], AF.Relu, [pres], [junk])
                    tt(u[:, half * 4 + j, :ntok], junk[:, :ntok], junk[:, :ntok], ALU.mult, [junk], [u])
                proj(w_up[l], DC, hbk * HB + half * 512, 512, "FM", lambda kc: bufA[:, kc, :ntok], bufA_R, None, ev_u)
            for g in range(8):
                def ev_d(j, pap, pres, g=g):
                    c = g * 4 + j
                    for (c0, n, s) in segs:
                        stt(xT[:, c, c0:c0 + n], pap[:, c0:c0 + n], modk(l, 5, c, s), xT[:, c, c0:c0 + n], ALU.mult, ALU.add,
                            [pres, modv[l], xT], [xT])
                proj(w_down[l][hbk * HB:(hbk + 1) * HB, :], HB // 128, g * 512, 512, "FM", lambda kc, u=u: u[:, kc, :ntok],
                     lambda kc, u=u: [u], None, ev_d)


    stage = bufB[:].rearrange("p c t -> p (c t)").bitcast(F32)
    assert DC * TT // 2 >= D

    def load_x(tile):
        src = tile["x"]
        for bi, (c0, n) in enumerate([(i * 128, 128) for i in range(tile["ntok"] // 128)]):
            P.dma("sp", stage[:, :D], src[c0:c0 + 128, :], W=[bufB])
            for c in range(DC):
                bank = ps[c % 8]
                tr(bank[:, 0:128], stage[:, c * 128:(c + 1) * 128], IDF, [bufB, kf], [bank])
                cp(xT[:, c, c0:c0 + 128], bank[:, 0:128], [bank], [xT], eng=("act" if c % 2 else "dve"))

    def store_y(tile):
        ntok = tile["ntok"]
        act(bufA[:, :, :ntok], xT[:, :, :ntok], AF.Square, [xT], [bufA])
        for c in range(DC):
            mm(ps[0][:, :ntok], ONESB, bufA[:, c, :ntok], c == 0, c == DC - 1, [bufA, kb], [ps[0]])
        act(tmpn[:, :ntok], ps[0][:, :ntok], AF.Sqrt, [ps[0], cst], [tmpn], bias=CEPS, scale=1.0 / D)
        recip(rstd[:, :ntok], tmpn[:, :ntok], [tmpn], [rstd])
        for c in range(DC):
            stt(xT[:, c, :ntok], xT[:, c, :ntok], vecs[:, V_FIN + c:V_FIN + c + 1], rstd[:, :ntok], ALU.mult, ALU.mult,
                [xT, vecs, rstd], [xT])
        for bi in range(ntok // 128):
            c0 = bi * 128
            for c in range(DC):
                bank = ps[c % 8]
                tr(bank[:, 0:128], xT[:, c, c0:c0 + 128], IDF, [xT, kf], [bank])
                cp(stage[:, c * 128:(c + 1) * 128], bank[:, 0:128], [bank], [bufB], eng=("act" if c % 2 else "dve"))
            out_dmas.append(P.dma("sp", tile["y"][c0:c0 + 128, :], stage[:, :D], R=[bufB], W=[("ydram", id(tile), bi)]))

    tiles = []
    nt = SEQ // TT
    for i in range(nt):
        run = {"zero": i == 0, "blocks": list(range(TT // 128)), "key": "pstate", "okey": "pstate",
               "C": pC, "n": pn, "m": pm, "S": pS, "Co": pC, "no": pn, "mo": pm, "So": pS}
        tiles.append({"ntok": TT, "x": x_p[i * TT:(i + 1) * TT, :], "y": y_p[i * TT:(i + 1) * TT, :],
                      "segs": [(0, TT, 0)], "blocks": [(j * 128, 128) for j in range(TT // 128)], "runs": [run]})
    sruns = []
    for s in range(NSAMP):
        sruns.append({"zero": False, "blocks": [s], "key": "sstate_in", "okey": "sstate_out",
                      "C": sC_in[:, s], "n": sn_in[:, s], "m": sm_in[:, s], "S": sS_in[:, s],
                      "Co": sC[:, s], "no": sn[:, s], "mo": sm[:, s], "So": sS[:, s]})
    tiles.append({"ntok": NSAMP * SLEN, "x": x_s, "y": y_s, "segs": [(s * SLEN, SLEN, 1 + s) for s in range(NSAMP)],
                  "blocks": [(s * SLEN, SLEN) for s in range(NSAMP)], "runs": sruns})

    for tile in tiles:
        load_x(tile)
        for l in range(DEPTH):
            adaln(l, 0, tile["ntok"], tile["segs"])
            mixer(l, tile)
            adaln(l, 1, tile["ntok"], tile["segs"])
            ffn(l, tile)
        store_y(tile)

    P.emit([("dma", d) for d in out_dmas])
    es.close()
    return nc


def _consts():
    kf = np.zeros((128, 4, 128), np.float32)
    kf[:, 0, :] = np.eye(128)
    s = np.arange(128)[:, None]
    t = np.arange(128)[None, :]
    kf[:, 1, :] = (s <= t)
    kf[:, 2, :] = np.where(t <= s, 0.0, NEG)
    kf[:, 3, :] = 1.0
    kb = np.zeros((128, 4, 128), np.float32)
    kb[:, 0, :] = np.eye(128)
    kb[:, 1, :] = 1.0
    kb[:, 2, :] = ((s // 16) == (t // 16)) & (s <= t)
    kb[:, 3, :8] = ((np.arange(128)[:, None] // 16) == np.arange(8)[None, :])
    ksc = np.ones((128, TT), np.float32)
    ksc[:, ::16] = 0.0
    return kf, kb.astype(ml_dtypes.bfloat16), ksc


_CACHE = {}


def kernel(x_prompt, x_sample, state_mlstm_C, state_mlstm_n, state_mlstm_m, state_hgrn_S,
           c_prompt, c_sample, w_mod, b_mod, norm1_g, w_in, b_gate, lower_bounds,
           mlstm_norm_g, hgrn_norm_g, w_out, norm2_g, w_up, w_down, final_g, _ncores=8):
    f = lambda a: np.ascontiguousarray(np.asarray(a), dtype=np.float32)
    x_prompt, x_sample = f(x_prompt), f(x_sample)
    B, SEQ = x_prompt.shape[0], x_prompt.shape[1]
    ncores = _ncores
    if SEQ not in _CACHE:
        _CACHE[SEQ] = build_program(SEQ)
    nc = _CACHE[SEQ]
    kf, kb, ksc = _consts()
    shared = {
        "w_mod": f(w_mod), "b_mod": f(b_mod).reshape(DEPTH * 192, 128), "norm1_g": f(norm1_g).reshape(DEPTH * DC, 128),
        "w_in": f(w_in), "b_gate": f(b_gate), "lower_bounds": f(lower_bounds).reshape(DEPTH * 16, 128),
        "mlstm_norm_g": f(mlstm_norm_g), "hgrn_norm_g": f(hgrn_norm_g).reshape(DEPTH * 16, 128),
        "w_out": f(w_out), "norm2_g": f(norm2_g).reshape(DEPTH * DC, 128), "w_up": f(w_up), "w_down": f(w_down),
        "final_g": f(final_g).reshape(DC, 128), "k_f32": kf, "k_bf": kb, "k_scan": ksc,
    }
    sC_, sn_, sm_, sS_ = f(state_mlstm_C), f(state_mlstm_n), f(state_mlstm_m), f(state_hgrn_S)
    c_prompt, c_sample = f(c_prompt), f(c_sample)
    in_maps = []
    for c in range(ncores):
        pb = c % B
        ss = [(NSAMP * c + i) % x_sample.shape[0] for i in range(NSAMP)]
        m = dict(shared)
        m["x_p"] = x_prompt[pb]
        m["x_s"] = np.ascontiguousarray(x_sample[ss].reshape(NSAMP * SLEN, D))
        m["c_all"] = np.ascontiguousarray(np.concatenate([c_prompt[pb:pb + 1], c_sample[ss]], axis=0).reshape(3 * DC, 128))
        m["sC_in"] = np.ascontiguousarray(sC_[:, ss])
        m["sn_in"] = np.ascontiguousarray(sn_[:, ss])
        m["sm_in"] = np.ascontiguousarray(sm_[:, ss])
        m["sS_in"] = np.ascontiguousarray(sS_[:, ss])
        in_maps.append(m)
    res = run_bass_kernel_spmd(nc, in_maps, core_ids=list(range(ncores)))
    R = res.results
    nS = x_sample.shape[0]
    y_prompt = np.stack([R[b]["y_p"] for b in range(B)]) if ncores >= B else np.stack([R[0]["y_p"]])
    def gather_s(name, axis_first):
        outs = [None] * nS
        for c in range(ncores):
            for i in range(NSAMP):
                sidx = NSAMP * c + i
                if sidx < nS and outs[sidx] is None:
                    a = R[c][name]
                    outs[sidx] = a[i * SLEN:(i + 1) * SLEN] if axis_first else a[:, i]
        return outs
    ys = gather_s("y_s", True)
    have = [o for o in ys if o is not None]
    y_sample = np.stack(have)
    nb = min(B, ncores)
    pCo = np.stack([R[b]["pC"] for b in range(nb)], axis=1)
    pno = np.stack([R[b]["pn"] for b in range(nb)], axis=1)
    pmo = np.stack([R[b]["pm"] for b in range(nb)], axis=1)
    pSo = np.stack([R[b]["pS"] for b in range(nb)], axis=1)
    def st(name):
        o = [x for x in gather_s(name, False) if x is not None]
        return np.stack(o, axis=1)
    return (y_prompt.astype(np.float32), y_sample.astype(np.float32), pCo, pno, pmo, pSo,
            st("sC"), st("sn"), st("sm"), st("sS"))
```

```python
import numpy as np
import ml_dtypes
import concourse.bass as bass
import concourse.mybir as mybir
from concourse.bass_utils import run_bass_kernel_spmd

F32 = mybir.dt.float32
BF16 = mybir.dt.bfloat16
AF = mybir.ActivationFunctionType
ALU = mybir.AluOpType
AX = mybir.AxisListType

D = 4096
DC = 32
DEPTH = 2
M_HEADS, M_DK, M_DV = 4, 256, 512
H_HEADS, H_DK = 16, 128
D_FF = 16384
IN_COLS = 14344
C_MQ, C_MK, C_MV, C_MO, C_MG, C_HQ, C_HF, C_HI, C_HG = 0, 1024, 2048, 4096, 6144, 6152, 8200, 10248, 12296
EPS = 1e-6
NEG = -1e30
TT = 256
KC = 4
NRING = 8
NSAMP = 2
SLEN = 64


Q_W = "sp"
Q_O = "pool"


class Prog:
    ENG = ("pe", "act", "dve", "pool", "sp")

    def __init__(self, nc):
        self.nc = nc
        self.streams = {e: [] for e in self.ENG}
        self.res = {}
        self.dma_sems = []
        self.dma_rr = 0
        self.dma_last = {}
        self.named_sems = {}

    @staticmethod
    def key(r):
        if isinstance(r, (str, tuple)):
            return r
        return r.name

    def _deps(self, R, W):
        deps = []
        for r in R:
            st = self.res.setdefault(self.key(r), {"w": None, "r": {}, "rd": []})
            if st["w"] is not None:
                deps.append(st["w"])
        for w in W:
            st = self.res.setdefault(self.key(w), {"w": None, "r": {}, "rd": []})
            if st["w"] is not None:
                deps.append(st["w"])
            deps.extend(st["r"].values())
            deps.extend(st["rd"])
        return deps

    def _commit(self, ev, R, W):
        for r in R:
            st = self.res[self.key(r)]
            if ev[0] == "op":
                st["r"][ev[1]["eng"]] = ev
            else:
                st["rd"].append(ev)
        for w in W:
            st = self.res[self.key(w)]
            st["w"] = ev
            st["r"] = {}
            st["rd"] = []

    @staticmethod
    def _psum_fix(self_key, R, W):
        R2, W2 = [], list(W)
        for r in R:
            k = self_key(r)
            if isinstance(k, str) and k.startswith("ps") and k[2:].isdigit():
                W2.append(r)
            else:
                R2.append(r)
        return R2, W2

    def op(self, eng, fn, R=(), W=()):
        R, W = self._psum_fix(self.key, R, W)
        deps = self._deps(R, W)
        ins = {"eng": eng, "fn": fn, "deps": deps, "kind": "op", "needed": False, "ticket": None}
        self.streams[eng].append(ins)
        self._commit(("op", ins), R, W)
        return ins

    def dma(self, eng, out, in_, R=(), W=(), semkey=None, **kw):
        deps = self._deps(R, W)
        if semkey is None:
            semkey = ("rr", self.dma_rr % 24)
            self.dma_rr += 1
        prev = self.dma_last.get(semkey)
        if prev is not None:
            deps.append(("dma", prev))
        cnt = (prev["count"] if prev else 0) + 16
        ins = {"eng": eng, "out": out, "in_": in_, "deps": deps, "kind": "dma", "semkey": semkey,
               "count": cnt, "kw": kw}
        self.dma_last[semkey] = ins
        self.streams[eng].append(ins)
        self._commit(("dma", ins), R, W)
        return ins

    def emit(self, final_waits):
        nc = self.nc
        for e in self.ENG:
            for ins in self.streams[e]:
                for kind, d in ins["deps"]:
                    if kind == "op":
                        d["needed"] = True
        for e in self.ENG:
            t = 0
            for ins in self.streams[e]:
                if ins["kind"] == "op" and ins["needed"]:
                    t += 1
                    ins["ticket"] = t
        import contextlib
        with contextlib.ExitStack() as es:
            esem = {e: es.enter_context(nc.semaphore("sem_" + e)) for e in self.ENG}
            dsem = {}
            for e in self.ENG:
                for ins in self.streams[e]:
                    if ins["kind"] == "dma" and ins["semkey"] not in dsem:
                        dsem[ins["semkey"]] = es.enter_context(nc.semaphore("dsem%d" % len(dsem)))
            block = es.enter_context(nc.Block())

            def run(engname, E):
                seen = {}
                def wait(sem_id, sem, val):
                    if seen.get(sem_id, 0) >= val:
                        return
                    seen[sem_id] = val
                    E.wait_ge(sem, val)
                for ins in self.streams[engname]:
                    for kind, d in ins["deps"]:
                        if kind == "op":
                            if d["eng"] == engname and engname == "pe":
                                continue
                            wait(d["eng"], esem[d["eng"]], d["ticket"])
                        else:
                            wait(d["semkey"], dsem[d["semkey"]], d["count"])
                    if ins["kind"] == "op":
                        r = ins["fn"](E)
                        if ins["needed"]:
                            r.then_inc(esem[engname], 1)
                    else:
                        E.dma_start(out=ins["out"], in_=ins["in_"], **ins["kw"]).then_inc(dsem[ins["semkey"]], 16)
                if engname == "sp":
                    for kind, d in final_waits:
                        if kind == "op":
                            wait(d["eng"], esem[d["eng"]], d["ticket"])
                        else:
                            wait(d["semkey"], dsem[d["semkey"]], d["count"])

            @block.tensor
            def _(E):
                run("pe", E)

            @block.scalar
            def _(E):
                run("act", E)

            @block.vector
            def _(E):
                run("dve", E)

            @block.gpsimd
            def _(E):
                run("pool", E)

            @block.sync
            def _(E):
                run("sp", E)


def build_program(SEQ):
    nc = bass.Bass("TRN2", target_bir_lowering=False)
    P = Prog(nc)
    import contextlib
    es = contextlib.ExitStack()

    def pbc(ap1d, n):
        return ap1d.rearrange("(o n) -> o n", o=1).broadcast_to([128, n])

    def dram(name, shape, kind="ExternalInput", dt=F32):
        return nc.dram_tensor(name, list(shape), dt, kind=kind).ap()

    def sb(name, shape, dt=F32):
        return es.enter_context(nc.sbuf_tensor(name, list(shape), dt))

    x_p = dram("x_p", [SEQ, D])
    x_s = dram("x_s", [NSAMP * SLEN, D])
    c_all = dram("c_all", [3 * DC, 128])
    sC_in = dram("sC_in", [DEPTH, NSAMP, M_HEADS, M_DK, M_DV])
    sn_in = dram("sn_in", [DEPTH, NSAMP, M_HEADS, M_DK])
    sm_in = dram("sm_in", [DEPTH, NSAMP, M_HEADS])
    sS_in = dram("sS_in", [DEPTH, NSAMP, H_HEADS, H_DK, H_DK])
    w_mod = dram("w_mod", [DEPTH, D, 6 * D])
    b_mod = dram("b_mod", [DEPTH * 192, 128])
    n1g = dram("norm1_g", [DEPTH * DC, 128])
    w_in = dram("w_in", [DEPTH, D, IN_COLS])
    b_gate = dram("b_gate", [DEPTH, 8])
    lowb = dram("lower_bounds", [DEPTH * 16, 128])
    mng = dram("mlstm_norm_g", [DEPTH, 2048])
    hng = dram("hgrn_norm_g", [DEPTH * 16, 128])
    w_out = dram("w_out", [DEPTH, D, D])
    n2g = dram("norm2_g", [DEPTH * DC, 128])
    w_up = dram("w_up", [DEPTH, D, D_FF])
    w_down = dram("w_down", [DEPTH, D_FF, D])
    fing = dram("final_g", [DC, 128])
    k_f32 = dram("k_f32", [128, 4, 128])
    k_bf = dram("k_bf", [128, 4, 128], dt=BF16)
    k_scan = dram("k_scan", [128, TT])

    wb_in = dram("wb_in", [DEPTH, D, IN_COLS], kind="Internal", dt=BF16)
    wb_out = dram("wb_out", [DEPTH, D, D], kind="Internal", dt=BF16)
    wb_up = dram("wb_up", [DEPTH, D, D_FF], kind="Internal", dt=BF16)
    wb_down = dram("wb_down", [DEPTH, D_FF, D], kind="Internal", dt=BF16)

    y_p = dram("y_p", [SEQ, D], kind="ExternalOutput")
    y_s = dram("y_s", [NSAMP * SLEN, D], kind="ExternalOutput")
    pC = dram("pC", [DEPTH, M_HEADS, M_DK, M_DV], kind="ExternalOutput")
    pn = dram("pn", [DEPTH, M_HEADS, M_DK], kind="ExternalOutput")
    pm = dram("pm", [DEPTH, M_HEADS], kind="ExternalOutput")
    pS = dram("pS", [DEPTH, H_HEADS, H_DK, H_DK], kind="ExternalOutput")
    sC = dram("sC", [DEPTH, NSAMP, M_HEADS, M_DK, M_DV], kind="ExternalOutput")
    sn = dram("sn", [DEPTH, NSAMP, M_HEADS, M_DK], kind="ExternalOutput")
    sm = dram("sm", [DEPTH, NSAMP, M_HEADS], kind="ExternalOutput")
    sS = dram("sS", [DEPTH, NSAMP, H_HEADS, H_DK, H_DK], kind="ExternalOutput")

    xT = sb("xT", [128, DC, TT])
    bufA = sb("bufA", [128, DC, TT], BF16)
    bufB = sb("bufB", [128, DC, TT], BF16)
    ring = [sb("ring%d" % i, [128, KC, 512], BF16) for i in range(NRING)]
    kf = sb("kf", [128, 4, 128])
    kb = sb("kb", [128, 4, 128], BF16)
    kscan = sb("kscan", [128, TT])
    cst = sb("cst", [128, 4])
    modv = [sb("modv%d" % l, [128, 192, 3]) for l in range(DEPTH)]
    A1 = [sb("A1_%d" % l, [128, DC, 3]) for l in range(DEPTH)]
    A2 = [sb("A2_%d" % l, [128, DC, 3]) for l in range(DEPTH)]
    vecs = sb("vecs", [128, 2 * DC + 2 * DC + DC + 2 * 16 + 2 * 16 + 192 * 2 + 96])
    V_N1, V_N2, V_FIN, V_LB, V_HG, V_BM, V_C = 0, 64, 128, 160, 192, 224, 224 + 384
    vstage = sb("vstage", [128, 128])
    cT = sb("cT", [128, DC, 3], BF16)
    lbv = sb("lbv", [128, 2, 16, 3])
    rstd = sb("rstd", [128, TT])
    tmpn = sb("tmpn", [128, TT])
    bg_row = sb("bg_row", [128, DEPTH, 8])
    qT = [sb("qT%d" % i, [128, 2, TT], BF16) for i in range(2)]
    kT = [sb("kT%d" % i, [128, 2, TT], BF16) for i in range(2)]
    k_tok = [sb("k_tok%d" % i, [128, 2, 256], BF16) for i in range(2)]
    v_tok = [sb("v_tok%d" % i, [128, 2, 512], BF16) for i in range(2)]
    og = [sb("og%d" % i, [128, 2, 512], BF16) for i in range(2)]
    mg_row = [sb("mg_row%d" % i, [128, 512]) for i in range(2)]
    gates = sb("gates", [128, 2, 8])
    lf = sb("lf", [128, 2, 4])
    Cst = sb("Cst", [128, 2, 512])
    Cbf = sb("Cbf", [128, 2, 512], BF16)
    nst = sb("nst", [128, 2])
    nbf = sb("nbf", [128, 2], BF16)
    mst = sb("mst", [128, 1])
    sm_small = sb("sm_small", [128, 24])
    Dm = sb("Dm", [128, 128])
    Em = sb("Em", [128, 128])
    Sm = sb("Sm", [128, 128], BF16)
    SmT = sb("SmT", [128, 128], BF16)
    diagu = sb("diagu", [128, 128])
    hsum = sb("hsum", [128, 512])
    numt = sb("numt", [128, 512])
    junk = sb("junk", [128, 512])
    otok = sb("otok", [128, 512], BF16)
    ks = sb("ks", [128, 256], BF16)
    HH = 4
    hqT = [sb("hqT%d" % i, [128, HH, TT], BF16) for i in range(2)]
    sgT = [sb("sgT%d" % i, [128, HH, TT]) for i in range(2)]
    hgT = [sb("hgT%d" % i, [128, HH, TT], BF16) for i in range(2)]
    hv_tok = [sb("hv_tok%d" % i, [128, 2, HH * 128], BF16) for i in range(2)]
    Sst = [sb("Sst%d" % h, [128, 128]) for h in range(HH)]
    Sbf = [sb("Sbf%d" % h, [128, 128], BF16) for h in range(HH)]
    fT = sb("fT", [128, TT])
    gT = sb("gT", [128, TT])
    GT = sb("GT", [128, TT])
    kkT = sb("kkT", [128, TT])
    eT = sb("eT", [128, TT])
    kx32 = sb("kx32", [128, TT])
    qxT = [sb("qxT%d" % i, [128, TT], BF16) for i in range(HH)]
    kxT = [sb("kxT%d" % i, [128, TT], BF16) for i in range(HH)]
    kdT = [sb("kdT%d" % i, [128, TT], BF16) for i in range(HH)]
    eGl = [sb("eGl%d" % i, [128, TT // 16]) for i in range(HH)]
    kd_tok = sb("kd_tok", [128, 128], BF16)
    kdm = [sb("kdm%d" % i, [128, 8, 128], BF16) for i in range(HH)]
    Am = sb("Am", [128, 128], BF16)
    oT = sb("oT", [128, 128])
    osq = sb("osq", [128, 128], BF16)
    orst = sb("orst", [128, 128])
    uT = [sb("uT%d" % i, [128, 8, TT], BF16) for i in range(2)]


    ps = [es.enter_context(nc.psum_tensor("ps%d" % i, [128, 512], F32)) for i in range(8)]

    IDF, TRIU, MNEG, ONESF = kf[:, 0, :], kf[:, 1, :], kf[:, 2, :], kf[:, 3, :]
    IDB, ONESB, HMASK, RMASK = kb[:, 0, :], kb[:, 1, :], kb[:, 2, :], kb[:, 3, :]
    CEPS, CONE, CZERO = cst[:, 0:1], cst[:, 1:2], cst[:, 2:3]

    def act(out, in_, func, R, W, bias=None, scale=1.0):
        kw = {}
        if bias is not None:
            kw["bias"] = bias
        return P.op("act", lambda E: E.activation(out=out, in_=in_, func=func, scale=scale, **kw), R, W)

    def ts(out, in0, s1, s2, op0, op1, R, W, eng="dve"):
        return P.op(eng, lambda E: E.tensor_scalar(out=out, in0=in0, scalar1=s1, scalar2=s2, op0=op0, op1=op1), R, W)

    def stt(out, in0, s, in1, op0, op1, R, W):
        return P.op("dve", lambda E: E.scalar_tensor_tensor(out=out, in0=in0, scalar=s, in1=in1, op0=op0, op1=op1), R, W)

    def tt(out, in0, in1, op, R, W, eng="dve"):
        return P.op(eng, lambda E: E.tensor_tensor(out=out, in0=in0, in1=in1, op=op), R, W)

    def red(out, in_, op, R, W):
        return P.op("dve", lambda E: E.tensor_reduce(out=out, in_=in_, axis=AX.X, op=op), R, W)

    def recip(out, in_, R, W):
        return P.op("dve", lambda E: E.reciprocal(out=out, in_=in_), R, W)

    def cp(out, in_, R, W, eng="dve"):
        if eng == "act":
            return P.op("act", lambda E: E.copy(out=out, in_=in_), R, W)
        return P.op(eng, lambda E: E.tensor_copy(out=out, in_=in_), R, W)

    def mm(out, lhsT, rhs, start, stop, R, W):
        return P.op("pe", lambda E: E.matmul(out, lhsT, rhs, start=start, stop=stop), R, W)

    def tr(out, in_, ident, R, W):
        return P.op("pe", lambda E: E.transpose(out, in_, ident), R, W)

    def memset(ap, v, W, eng="dve"):
        return P.op(eng, lambda E: E.memset(ap, v), (), W)

    P.dma(Q_O, kf[:], k_f32, W=[kf])
    P.dma(Q_O, kb[:], k_bf, W=[kb])
    P.dma(Q_O, kscan[:], k_scan, W=[kscan])
    memset(cst[:, 0:1], EPS, [cst])
    memset(cst[:, 1:2], 1.0, [cst])
    memset(cst[:, 2:3], 0.0, [cst])

    def load_vec_fm(src_rows, nrows, dst):
        P.dma(Q_O, vstage[:nrows, :], src_rows, W=[vstage])
        tr(ps[0][:, :nrows], vstage[:nrows, :], IDF[:nrows, :nrows], [vstage, kf], [ps[0]])
        cp(dst, ps[0][:, :nrows], [ps[0]], [vecs])

    load_vec_fm(n1g, 64, vecs[:, V_N1:V_N1 + 64])
    load_vec_fm(n2g, 64, vecs[:, V_N2:V_N2 + 64])
    load_vec_fm(fing, 32, vecs[:, V_FIN:V_FIN + 32])
    load_vec_fm(lowb, 32, vecs[:, V_LB:V_LB + 32])
    load_vec_fm(hng, 32, vecs[:, V_HG:V_HG + 32])
    for i in range(3):
        load_vec_fm(b_mod[i * 128:(i + 1) * 128, :], 128, vecs[:, V_BM + i * 128:V_BM + (i + 1) * 128])
    load_vec_fm(c_all, 96, vecs[:, V_C:V_C + 96])
    act(cT[:].rearrange("p c s -> p s c"), vecs[:, V_C:V_C + 96].rearrange("p (s c) -> p s c", s=3), AF.Silu, [vecs], [cT])
    l0 = vecs[:, V_LB:V_LB + 16]
    l1 = vecs[:, V_LB + 16:V_LB + 32]
    tt(tmpn[:, 0:16], l1, l0, ALU.subtract, [vecs], [tmpn])
    act(lbv[:, 1, :, 0], tmpn[:, 0:16], AF.Sigmoid, [tmpn], [lbv])
    act(tmpn[:, 16:32], tmpn[:, 0:16], AF.Sigmoid, [tmpn], [tmpn], scale=-1.0)
    tt(lbv[:, 0, :, 0], tmpn[:, 16:32], tmpn[:, 16:32], ALU.subtract, [tmpn], [lbv])
    for l in range(DEPTH):
        ts(lbv[:, l, :, 1], lbv[:, l, :, 0], -1.0, 1.0, ALU.mult, ALU.add, [lbv], [lbv])
        ts(lbv[:, l, :, 2], lbv[:, l, :, 0], 1.0, -1.0, ALU.mult, ALU.add, [lbv], [lbv])
    P.dma(Q_O, bg_row[:].rearrange("p l g -> p (l g)"),
          b_gate.rearrange("(o l) g -> o (l g)", o=1).broadcast_to([128, DEPTH * 8]), W=[bg_row])

    ring_state = {"i": 0}

    def load_w(Wl, kc0, nk, col0, ncols):
        i = ring_state["i"]
        ring_state["i"] += 1
        slot = i % NRING
        src = Wl["ap"].rearrange("(kc p) c -> p kc c", p=128)[:, kc0:kc0 + nk, col0:col0 + ncols]
        if Wl.get("cast"):
            P.dma("pool", ring[slot][:, :nk, :ncols], src, W=[ring[slot]], semkey=("ring", slot))
        else:
            rk = [("wb",) + Wl["key"] + (Wl["kbase"] + kc0 + i,) for i in range(nk)]
            P.dma(Q_W, ring[slot][:, :nk, :ncols], src, R=rk, W=[ring[slot]], semkey=("ring", slot))
        return ring[slot]

    def cast_weights(l):
        for (src, dst, name, K) in ((w_in, wb_in, "in", D), (w_out, wb_out, "out", D), (w_up, wb_up, "up", D),
                                    (w_down, wb_down, "down", D_FF)):
            for r in range(K // 128):
                P.dma("pool", dst[l, r * 128:(r + 1) * 128, :], src[l, r * 128:(r + 1) * 128, :],
                      W=[("wb", name, l, r)], max_dma_last_dim=4096)

    bank_rr = {"i": 0}

    def proj_g(Wl, nkc, col0, ncols, mode, actT, actR, blocks, evac, base=None):
        if base is None:
            base = 4 * (bank_rr["i"] % 2)
            bank_rr["i"] += 1
        if mode == "FM":
            nout = (ncols + 127) // 128
        else:
            nout = len(blocks)
        assert nout <= 4
        for kg in range(0, nkc, KC):
            nk = min(KC, nkc - kg)
            slot = load_w(Wl, kg, nk, col0, ncols)
            for kk in range(nk):
                kc = kg + kk
                a = actT(kc)
                for j in range(nout):
                    if mode == "FM":
                        cw = min(128, ncols - j * 128)
                        ntok = a.shape[-1]
                        mm(ps[base + j][:cw, :ntok], slot[:, kk, j * 128:j * 128 + cw], a,
                           kc == 0, kc == nkc - 1, [slot] + actR(kc), [ps[base + j]])
                    else:
                        c0, n = blocks[j]
                        mm(ps[base + j][:n, :ncols], a[:, c0:c0 + n], slot[:, kk, :ncols],
                           kc == 0, kc == nkc - 1, [slot] + actR(kc), [ps[base + j]])
            yield
        for j in range(nout):
            if mode == "FM":
                cw = min(128, ncols - j * 128)
                evac(j, ps[base + j][:cw, :], ps[base + j])
            else:
                c0, n = blocks[j]
                evac(j, ps[base + j][:n, :ncols], ps[base + j])
        yield

    def drain(g):
        for _ in g:
            pass

    def proj(*a, **k):
        drain(proj_g(*a, **k))

    def par(ga, gb, ra=1, rb=1):
        alive_a = alive_b = True
        while alive_a or alive_b:
            for _ in range(ra):
                if alive_a:
                    try:
                        next(ga)
                    except StopIteration:
                        alive_a = False
            for _ in range(rb):
                if alive_b:
                    try:
                        next(gb)
                    except StopIteration:
                        alive_b = False


    def mod_phase():
        for l in range(DEPTH):
            for g in range(48):
                def ev(j, pap, pres, l=l, g=g):
                    ch = g * 4 + j
                    bm = vecs[:, V_BM + l * 192 + ch:V_BM + l * 192 + ch + 1]
                    ts(modv[l][:, ch, :], pap[:, :3], bm, None, ALU.add, ALU.bypass, [pres, vecs], [modv[l]])
                proj({"ap": w_mod[l], "cast": True}, DC, g * 512, 512, "FM", lambda kc: cT[:, kc, :], lambda kc: [cT], None, ev)
            for (Ax, voff, sck) in ((A1, V_N1, 1), (A2, V_N2, 4)):
                for s in range(3):
                    stt(Ax[l][:, :, s], modv[l][:, sck * 32:(sck + 1) * 32, s], 1.0, vecs[:, voff + l * 32:voff + (l + 1) * 32],
                        ALU.add, ALU.mult, [modv[l], vecs], [Ax[l]])

    def modk(l, kind, c, s):
        return modv[l][:, kind * 32 + c, s:s + 1]

    def adaln(l, which, ntok, segs):
        act(bufA[:, :, :ntok], xT[:, :, :ntok], AF.Square, [xT], [bufA])
        for c in range(DC):
            mm(ps[0][:, :ntok], ONESB, bufA[:, c, :ntok], c == 0, c == DC - 1, [bufA, kb], [ps[0]])
        act(tmpn[:, :ntok], ps[0][:, :ntok], AF.Sqrt, [ps[0], cst], [tmpn], bias=CEPS, scale=1.0 / D)
        recip(rstd[:, :ntok], tmpn[:, :ntok], [tmpn], [rstd])
        if which == 0:
            Ax, shk = A1, 0
        elif which == 1:
            Ax, shk = A2, 3
        for c in range(DC):
            for (c0, n, s) in segs:
                stt(tmpn[:, c0:c0 + n], xT[:, c, c0:c0 + n], Ax[l][:, c, s:s + 1], rstd[:, c0:c0 + n],
                    ALU.mult, ALU.mult, [xT, Ax[l], rstd], [tmpn])
                act(bufA[:, c, c0:c0 + n], tmpn[:, c0:c0 + n], AF.Identity, [tmpn, modv[l]], [("bufA", c)],
                    bias=modk(l, shk, c, s))
        return

    def bufA_R(kc):
        return [bufA, ("bufA", kc)]

    out_dmas = []
    RB = 4

    def mlstm_chunk_g(l, h, c0, n, bi):
        zb = h % 2
        q_, k_, kt_, vt_, og_, mgr = qT[zb], kT[zb], k_tok[zb], v_tok[zb], og[zb], mg_row[zb]
        sml = sm_small
        col = lambda i: sml[:n, i:i + 1]
        colP = lambda i: sml[:, i:i + 1]
        ig = gates[:n, bi, h:h + 1]
        lfv = lf[:n, bi, h:h + 1]
        p0, p1, p2, p3 = ps[0], ps[1], ps[2], ps[3]
        pst = p1[:].bitcast(BF16)
        mm(p0[:n, 200:201], TRIU[:n, :n], lfv, True, True, [kf, lf], [p0])
        mm(p0[:, 201:202], ONESF[:n, :], lfv, True, True, [kf, lf], [p0])
        for c in range(2):
            mm(p0[:n, :n], q_[:, c, c0:c0 + n], k_[:, c, c0:c0 + n], c == 0, c == 1, [q_, k_], [p0])
        for c in range(2):
            mm(p2[:n, :], q_[:, c, c0:c0 + n], Cbf[:, c, :], c == 0, c == 1, [q_, Cbf], [p2])
        for c in range(2):
            mm(p0[:n, 202:203], q_[:, c, c0:c0 + n], nbf[:, c:c + 1], c == 0, c == 1, [q_, nbf], [p0])
        yield
        cp(col(0), p0[:n, 200:201], [p0], [sml])
        cp(colP(1), p0[:, 201:202], [p0], [sml])
        tt(col(2), ig, col(0), ALU.subtract, [gates, sml], [sml])
        ts(diagu[:n, :n], IDF[:n, :n], col(2), None, ALU.mult, ALU.bypass, [kf, sml], [diagu])
        cp(col(21), p0[:n, 202:203], [p0], [sml])
        cp(Dm[:n, :n], p0[:n, :n], [p0], [Dm], eng="act")
        mm(p0[:, 256:256 + n], ONESF[:n, :], diagu[:n, :n], True, True, [kf, diagu], [p0])
        yield
        stt(Em[:n, :n], p0[:n, 256:256 + n], col(0), MNEG[:n, :n], ALU.add, ALU.add, [p0, sml, kf], [Em])
        red(col(3), Em[:n, :n], ALU.max, [Em], [sml])
        red(colP(4), p0[:, 256:256 + n], ALU.max, [p0], [sml])
        tt(col(5), col(0), mst[:n, :], ALU.add, [sml, mst], [sml])
        tt(col(6), col(5), col(3), ALU.max, [sml], [sml])
        ts(col(7), col(6), -1.0, None, ALU.mult, ALU.bypass, [sml], [sml])
        act(col(8), col(5), AF.Exp, [sml], [sml], bias=col(7))
        act(Em[:n, :n], Em[:n, :n], AF.Exp, [Em, sml], [Em], bias=col(7))
        tt(Sm[:n, :n], Dm[:n, :n], Em[:n, :n], ALU.mult, [Dm, Em], [Sm])
        red(col(9), Sm[:n, :n], ALU.add, [Sm], [sml])
        tr(pst[:n, :n], Sm[:n, :n], IDB[:n, :n], [Sm, kb], [p1])
        yield
        cp(SmT[:n, :n], pst[:n, :n], [p1], [SmT], eng="act")
        P.op("act", lambda E: E.activation(out=hsum[:n, :], in_=p2[:n, :], func=AF.Copy, scale=col(8)), [p2, sml], [hsum])
        mm(p3[:n, :], SmT[:n, :n], vt_[:n, bi, :], True, True, [SmT, vt_], [p3])
        yield
        tt(numt[:n, :], p3[:n, :], hsum[:n, :], ALU.add, [p3, hsum], [numt])
        stt(col(10), col(21), col(8), col(9), ALU.mult, ALU.add, [sml], [sml])
        ts(col(20), col(10), -1.0, None, ALU.mult, ALU.bypass, [sml], [sml])
        tt(col(10), col(10), col(20), ALU.max, [sml], [sml])
        act(col(11), col(6), AF.Exp, [sml], [sml], scale=-1.0)
        tt(col(10), col(10), col(11), ALU.max, [sml], [sml])
        recip(col(12), col(10), [sml], [sml])
        act(junk[:n, :], numt[:n, :], AF.Square, [numt], [junk])
        red(col(13), junk[:n, :], ALU.add, [junk], [sml])
        tt(col(14), col(12), col(12), ALU.mult, [sml], [sml])
        tt(col(14), col(14), col(13), ALU.mult, [sml], [sml])
        act(col(15), col(14), AF.Sqrt, [sml, cst], [sml], bias=CEPS[:n, :], scale=1.0 / M_DV)
        recip(col(15), col(15), [sml], [sml])
        tt(col(15), col(15), col(12), ALU.mult, [sml], [sml])
        tt(junk[:n, :], og_[:n, bi, :], mgr[:n, :], ALU.mult, [og_, mgr, junk], [junk])
        stt(otok[:n, :], numt[:n, :], col(15), junk[:n, :], ALU.mult, ALU.mult, [numt, sml, junk], [otok])
        tt(colP(16), mst[:, :], colP(4), ALU.max, [mst, sml], [sml])
        ts(colP(17), colP(16), -1.0, None, ALU.mult, ALU.bypass, [sml], [sml])
        act(colP(18), mst[:, :], AF.Exp, [mst, sml], [sml], bias=colP(17))
        act(col(19), col(2), AF.Exp, [sml], [sml], bias=col(17))
        tt(mst[:, :], colP(16), colP(1), ALU.add, [sml], [mst])
        ts(ks[:n, :], kt_[:n, bi, :], col(19), None, ALU.mult, ALU.bypass, [kt_, sml], [ks])
        for j in range(4):
            tr(pst[:, 512 + j * 128:512 + j * 128 + n], otok[:n, j * 128:(j + 1) * 128], IDB[:n, :n], [otok, kb], [p1])
        for c in range(2):
            pc = (p2, p3)[c]
            mm(pc[:, :], ks[:n, c * 128:(c + 1) * 128], vt_[:n, bi, :], True, True, [ks, vt_], [pc])
            mm(p0[:, 204 + c:205 + c], ks[:n, c * 128:(c + 1) * 128], ONESB[:n, 0:1], True, True, [ks, kb], [p0])
        yield
        cp(bufB[:, h * 4:(h + 1) * 4, c0:c0 + n], pst[:, 512:1024].rearrange("p (j t) -> p j t", j=4)[:, :, :n],
           [p1], [bufB], eng="act")
        for c in range(2):
            pc = (p2, p3)[c]
            stt(Cst[:, c, :], Cst[:, c, :], colP(18), pc[:, :], ALU.mult, ALU.add, [Cst, sml, pc], [Cst])
            cp(Cbf[:, c, :], Cst[:, c, :], [Cst], [Cbf], eng="act")
        stt(nst[:, :], nst[:, :], colP(18), p0[:, 204:206], ALU.mult, ALU.add, [nst, sml, p0], [nst])
        cp(nbf[:, :], nst[:, :], [nst], [nbf])
        yield

    def mlstm_state_load(l, h, run):
        if run["zero"]:
            memset(Cst[:], 0.0, [Cst])
            memset(Cbf[:], 0.0, [Cbf])
            memset(nst[:], 0.0, [nst])
            memset(nbf[:], 0.0, [nbf])
            memset(mst[:], 0.0, [mst])
            return
        Csrc, nsrc, msrc = run["C"], run["n"], run["m"]
        P.dma(Q_O, Cst[:], Csrc[l, h].rearrange("(c p) e -> p c e", p=128), R=[(run["key"], l, h, "C")], W=[Cst])
        P.dma(Q_O, nst[:], nsrc[l, h].rearrange("(c p) -> p c", p=128), R=[(run["key"], l, h, "n")], W=[nst],
              allow_slow_non_contiguous=True)
        P.dma(Q_O, mst[:], pbc(msrc[l, h:h + 1], 1), R=[(run["key"], l, h, "m")], W=[mst])
        cp(Cbf[:], Cst[:], [Cst], [Cbf], eng="act")
        cp(nbf[:], nst[:], [nst], [nbf])

    def mlstm_state_store(l, h, run):
        Cd, nd, md = run["Co"], run["no"], run["mo"]
        out_dmas.append(P.dma(Q_O, Cd[l, h].rearrange("(c p) e -> p c e", p=128), Cst[:], R=[Cst], W=[(run["okey"], l, h, "C")]))
        out_dmas.append(P.dma(Q_O, nd[l, h].rearrange("(c p) -> p c", p=128), nst[:], R=[nst], W=[(run["okey"], l, h, "n")],
                              allow_slow_non_contiguous=True))
        out_dmas.append(P.dma(Q_O, md[l, h:h + 1].rearrange("(o n) -> o n", o=1), mst[0:1, :], R=[mst], W=[(run["okey"], l, h, "m")]))

    def gates_proj(l, tile):
        ntok, blocks = tile["ntok"], tile["blocks"]
        hT = lambda kc: bufA[:, kc, :ntok]
        def ev_g(b, pap, pres):
            c0, n = blocks[b]
            tt(gates[:n, b, :], pap, bg_row[:n, l, :], ALU.add, [pres, bg_row], [gates])
            act(lf[:n, b, :], gates[:n, b, 4:8], AF.Exp, [gates], [lf], scale=-1.0)
            act(lf[:n, b, :], lf[:n, b, :], AF.Ln, [lf, cst], [lf], bias=CONE[:n, :])
            ts(lf[:n, b, :], lf[:n, b, :], -1.0, None, ALU.mult, ALU.bypass, [lf], [lf])
        proj({"ap": wb_in[l], "key": ("in", l), "kbase": 0}, DC, C_MG, 8, "TM", hT, bufA_R, blocks, ev_g, base=RB)

    def proj_m_g(l, h, tile):
        ntok, blocks = tile["ntok"], tile["blocks"]
        zb = h % 2
        Wl = {"ap": wb_in[l], "key": ("in", l), "kbase": 0}
        hT = lambda kc: bufA[:, kc, :ntok]
        P.dma(Q_O, mg_row[zb][:], pbc(mng[l, h * 512:(h + 1) * 512], 512), W=[mg_row[zb]])
        def ev_q(j, pap, pres):
            ts(qT[zb][:, j, :ntok], pap[:, :ntok], M_DK ** -0.5, None, ALU.mult, ALU.bypass, [pres], [qT[zb]])
        def ev_k(j, pap, pres):
            cp(kT[zb][:, j, :ntok], pap[:, :ntok], [pres], [kT[zb]], eng="act")
        def ev_kt(b, pap, pres):
            cp(k_tok[zb][:blocks[b][1], b, :], pap, [pres], [k_tok[zb]])
        def ev_v(b, pap, pres):
            cp(v_tok[zb][:blocks[b][1], b, :], pap, [pres], [v_tok[zb]], eng="act")
        def ev_o(b, pap, pres):
            act(og[zb][:blocks[b][1], b, :], pap, AF.Sigmoid, [pres], [og[zb]])
        yield from proj_g(Wl, DC, C_MQ + h * 256, 256, "FM", hT, bufA_R, None, ev_q, base=RB)
        yield from proj_g(Wl, DC, C_MK + h * 256, 256, "FM", hT, bufA_R, None, ev_k, base=RB)
        yield from proj_g(Wl, DC, C_MK + h * 256, 256, "TM", hT, bufA_R, blocks, ev_kt, base=RB)
        yield from proj_g(Wl, DC, C_MV + h * 512, 512, "TM", hT, bufA_R, blocks, ev_v, base=RB)
        yield from proj_g(Wl, DC, C_MO + h * 512, 512, "TM", hT, bufA_R, blocks, ev_o, base=RB)

    def rec_m_g(l, h, tile):
        blocks = tile["blocks"]
        for run in tile["runs"]:
            mlstm_state_load(l, h, run)
            yield
            for bi in run["blocks"]:
                c0, n = blocks[bi]
                yield from mlstm_chunk_g(l, h, c0, n, bi)
            mlstm_state_store(l, h, run)
            yield

    RK = [ps[1 + i // 2] for i in range(HH)]
    UK = [ps[3] for i in range(HH)]

    def hbarrier():
        return

    def proj_h_g(l, q, tile):
        ntok, blocks = tile["ntok"], tile["blocks"]
        zb = q % 2
        Wl = {"ap": wb_in[l], "key": ("in", l), "kbase": 0}
        hT = lambda kc: bufA[:, kc, :ntok]
        hb = q * HH
        def ev_q(j, pap, pres):
            act(hqT[zb][:, j, :ntok], pap[:, :ntok], AF.Silu, [pres], [hqT[zb]])
        def ev_f(j, pap, pres):
            act(sgT[zb][:, j, :ntok], pap[:, :ntok], AF.Sigmoid, [pres], [sgT[zb]])
        def ev_g(j, pap, pres):
            act(hgT[zb][:, j, :ntok], pap[:, :ntok], AF.Silu, [pres], [hgT[zb]])
        def ev_v(b, pap, pres):
            cp(hv_tok[zb][:blocks[b][1], b, :], pap, [pres], [hv_tok[zb]])
        yield from proj_g(Wl, DC, C_HQ + hb * 128, 512, "FM", hT, bufA_R, None, ev_q, base=RB)
        yield from proj_g(Wl, DC, C_HF + hb * 128, 512, "FM", hT, bufA_R, None, ev_f, base=RB)
        yield from proj_g(Wl, DC, C_HI + hb * 128, 512, "TM", hT, bufA_R, blocks, ev_v, base=RB)
        yield from proj_g(Wl, DC, C_HG + hb * 128, 512, "FM", hT, bufA_R, None, ev_g, base=RB)

    def rec_h_g(l, q, tile):
        ntok, blocks, runs = tile["ntok"], tile["blocks"], tile["runs"]
        zb = q % 2
        nb = ntok // 16
        hq_, sg_, hg_, hv_ = hqT[zb], sgT[zb], hgT[zb], hv_tok[zb]
        pA = ps[0]
        pAb = pA[:].bitcast(BF16)
        def Rap(i):
            return ps[1 + i // 2][:, (i % 2) * 256:(i % 2) * 256 + 256]
        def Uap(i):
            return ps[3][:, i * 128:(i + 1) * 128]
        hbarrier()
        for i in range(HH):
            h = q * HH + i
            lb, oml, moml = lbv[:, l, h, 0:1], lbv[:, l, h, 1:2], lbv[:, l, h, 2:3]
            sg = sg_[:, i, :ntok]
            ts(fT[:, :ntok], sg, oml, lb, ALU.mult, ALU.add, [sg_, lbv], [fT])
            act(gT[:, :ntok], fT[:, :ntok], AF.Ln, [fT], [gT])
            P.op("dve", lambda E: E.tensor_tensor_scan(out=GT[:, :ntok], data0=kscan[:, :ntok], data1=gT[:, :ntok],
                                                       initial=0.0, op0=ALU.mult, op1=ALU.add), [kscan, gT], [GT])
            ts(kkT[:, :ntok], sg, moml, oml, ALU.mult, ALU.add, [sg_, lbv], [kkT])
            act(eT[:, :ntok], GT[:, :ntok], AF.Exp, [GT], [eT])
            tt(qxT[i][:, :ntok], hq_[:, i, :ntok], eT[:, :ntok], ALU.mult, [hq_, eT], [qxT[i]])
            act(eT[:, :ntok], GT[:, :ntok], AF.Exp, [GT], [eT], scale=-1.0)
            tt(kx32[:, :ntok], kkT[:, :ntok], eT[:, :ntok], ALU.mult, [kkT, eT], [kx32])
            cp(kxT[i][:, :ntok], kx32[:, :ntok], [kx32], [kxT[i]], eng="act")
            GTv = GT[:, :ntok].rearrange("p (b k) -> p b k", k=16)
            act(eGl[i][:, :nb], GTv[:, :, 15], AF.Exp, [GT], [eGl[i]])
            tt(kdT[i][:, :ntok].rearrange("p (b k) -> p b k", k=16), kx32[:, :ntok].rearrange("p (b k) -> p b k", k=16),
               eGl[i][:, :nb].unsqueeze(2).broadcast_to([128, nb, 16]), ALU.mult, [kx32, eGl[i]], [kdT[i]])
            yield
        for run in runs:
            for i in range(HH):
                h = q * HH + i
                if run["zero"]:
                    memset(Sst[i][:], 0.0, [Sst[i]])
                    memset(Sbf[i][:], 0.0, [Sbf[i]])
                else:
                    P.dma(Q_O, Sst[i][:], run["S"][l, h], R=[(run["key"], l, h, "S")], W=[Sst[i]])
                    cp(Sbf[i][:], Sst[i][:], [Sst[i]], [Sbf[i]], eng="act")
            yield
            for bi in run["blocks"]:
                c0, n = blocks[bi]
                nsb = n // 16
                for i in range(HH):
                    vh = hv_[:n, bi, i * 128:(i + 1) * 128]
                    tr(pAb[:n, 512:640], kdT[i][:, c0:c0 + n], IDB, [kdT[i], kb], [pA])
                    mm(pA[:n, :n], kxT[i][:, c0:c0 + n], qxT[i][:, c0:c0 + n], True, True, [kxT[i], qxT[i]], [pA])
                    yield
                    cp(kd_tok[:n, :], pAb[:n, 512:640], [pA], [kd_tok], eng="act")
                    tt(kdm[i][:n, :nsb, :], kd_tok[:n, :].unsqueeze(1).broadcast_to([n, nsb, 128]),
                       RMASK[:n, :nsb].unsqueeze(2).broadcast_to([n, nsb, 128]), ALU.mult, [kd_tok, kb], [kdm[i]])
                    tt(Am[:n, :n], pA[:n, :n], HMASK[:n, :n], ALU.mult, [pA, kb], [Am])
                    mm(Rap(i)[:, 0:n], vh, Am[:n, :n], True, True, [hv_, Am], [RK[i]])
                    yield
                for sbk in range(nsb):
                    t0 = c0 + sbk * 16
                    gi = (c0 // 16) + sbk
                    for i in range(HH):
                        vh = hv_[:n, bi, i * 128:(i + 1) * 128]
                        mm(Rap(i)[:, 128 + sbk * 16:128 + sbk * 16 + 16], Sbf[i][:, :], qxT[i][:, t0:t0 + 16], True, True,
                           [Sbf[i], qxT[i]], [RK[i]])
                        mm(Uap(i), kdm[i][:n, sbk, :], vh, True, True, [kdm[i], hv_], [UK[i]])
                    for i in range(HH):
                        stt(Sst[i][:, :], Sst[i][:, :], eGl[i][:, gi:gi + 1], Uap(i), ALU.mult, ALU.add,
                            [Sst[i], eGl[i], UK[i]], [Sst[i]])
                        cp(Sbf[i][:, :], Sst[i][:, :], [Sst[i]], [Sbf[i]], eng="act")
                    yield
                for i in range(HH):
                    h = q * HH + i
                    cp(oT[:, :n], Rap(i)[:, 0:n], [RK[i]], [oT], eng="act")
                    tt(oT[:, :n], oT[:, :n], Rap(i)[:, 128:128 + n], ALU.add, [oT, RK[i]], [oT])
                    act(osq[:, :n], oT[:, :n], AF.Square, [oT], [osq])
                    mm(pA[:, 384:384 + n], ONESB, osq[:, :n], True, True, [kb, osq], [pA])
                    yield
                    act(orst[:, :n], pA[:, 384:384 + n], AF.Sqrt, [pA, cst], [orst], bias=CEPS, scale=1.0 / H_DK)
                    recip(orst[:, :n], orst[:, :n], [orst], [orst])
                    stt(oT[:, :n], oT[:, :n], vecs[:, V_HG + l * 16 + h:V_HG + l * 16 + h + 1], orst[:, :n], ALU.mult, ALU.mult,
                        [oT, vecs, orst], [oT])
                    tt(bufB[:, 16 + h, c0:c0 + n], oT[:, :n], hg_[:, i, c0:c0 + n], ALU.mult, [oT, hg_], [bufB])
            for i in range(HH):
                h = q * HH + i
                out_dmas.append(P.dma(Q_O, run["So"][l, h], Sst[i][:], R=[Sst[i]], W=[(run["okey"], l, h, "S")]))
            yield
        hbarrier()
        yield

    def out_proj_g(l, tile, kc0, nkc, base=None):
        ntok, segs = tile["ntok"], tile["segs"]
        Wl = {"ap": wb_out[l][kc0 * 128:(kc0 + nkc) * 128, :], "key": ("out", l), "kbase": kc0}
        for g in range(8):
            def ev(j, pap, pres, g=g):
                c = g * 4 + j
                for (c0, n, s) in segs:
                    stt(xT[:, c, c0:c0 + n], pap[:, c0:c0 + n], modk(l, 2, c, s), xT[:, c, c0:c0 + n], ALU.mult, ALU.add,
                        [pres, modv[l], xT], [xT])
            yield from proj_g(Wl, nkc, g * 512, 512, "FM", lambda kc: bufB[:, kc0 + kc, :ntok], lambda kc: [bufB], None, ev,
                              base=base)

    def mixer(l, tile):
        gates_proj(l, tile)
        pm = [proj_m_g(l, h, tile) for h in range(M_HEADS)]
        rm = [rec_m_g(l, h, tile) for h in range(M_HEADS)]
        NQ = H_HEADS // HH
        ph = [proj_h_g(l, q, tile) for q in range(NQ)]
        rh = [rec_h_g(l, q, tile) for q in range(NQ)]
        drain(pm[0])
        for h in range(M_HEADS - 1):
            par(rm[h], pm[h + 1], 1, 3)
        par(rm[M_HEADS - 1], ph[0], 1, 3)
        for q in range(NQ - 1):
            par(rh[q], ph[q + 1], 1, 1)
        par(rh[NQ - 1], out_proj_g(l, tile, 0, 28, base=RB), 1, 1)
        drain(out_proj_g(l, tile, 28, 4))

    def ffn(l, tile):
        ntok, segs = tile["ntok"], tile["segs"]
        HB = 1024
        for hbk in range(D_FF // HB):
            u = uT[hbk % 2]
            for half in range(2):
                def ev_u(j, pap, pres, u=u, half=half):
                    act(junk[:, :ntok], pap[:, :ntok], AF.Relu, [pres], [junk])
                    tt(u[:, half * 4 + j, :ntok], junk[:, :ntok], junk[:, :ntok], ALU.mult, [junk], [u])
                proj({"ap": wb_up[l], "key": ("up", l), "kbase": 0}, DC, hbk * HB + half * 512, 512, "FM", lambda kc: bufA[:, kc, :ntok], bufA_R, None, ev_u)
            for g in range(8):
                def ev_d(j, pap, pres, g=g):
                    c = g * 4 + j
                    for (c0, n, s) in segs:
                        stt(xT[:, c, c0:c0 + n], pap[:, c0:c0 + n], modk(l, 5, c, s), xT[:, c, c0:c0 + n], ALU.mult, ALU.add,
                            [pres, modv[l], xT], [xT])
                proj({"ap": wb_down[l][hbk * HB:(hbk + 1) * HB, :], "key": ("down", l), "kbase": hbk * HB // 128}, HB // 128, g * 512, 512, "FM", lambda kc, u=u: u[:, kc, :ntok],
                     lambda kc, u=u: [u], None, ev_d)


    stage = bufB[:].rearrange("p c t -> p (c t)").bitcast(F32)
    assert DC * TT // 2 >= D

    def load_x(tile):
        src = tile["x"]
        for bi, (c0, n) in enumerate([(i * 128, 128) for i in range(tile["ntok"] // 128)]):
            P.dma(Q_O, stage[:, :D], src[c0:c0 + 128, :], W=[bufB])
            for c in range(DC):
                bank = ps[c % 8]
                tr(bank[:, 0:128], stage[:, c * 128:(c + 1) * 128], IDF, [bufB, kf], [bank])
                cp(xT[:, c, c0:c0 + 128], bank[:, 0:128], [bank], [xT], eng=("act" if c % 2 else "dve"))

    def store_y(tile):
        ntok = tile["ntok"]
        act(bufA[:, :, :ntok], xT[:, :, :ntok], AF.Square, [xT], [bufA])
        for c in range(DC):
            mm(ps[0][:, :ntok], ONESB, bufA[:, c, :ntok], c == 0, c == DC - 1, [bufA, kb], [ps[0]])
        act(tmpn[:, :ntok], ps[0][:, :ntok], AF.Sqrt, [ps[0], cst], [tmpn], bias=CEPS, scale=1.0 / D)
        recip(rstd[:, :ntok], tmpn[:, :ntok], [tmpn], [rstd])
        for c in range(DC):
            stt(xT[:, c, :ntok], xT[:, c, :ntok], vecs[:, V_FIN + c:V_FIN + c + 1], rstd[:, :ntok], ALU.mult, ALU.mult,
                [xT, vecs, rstd], [xT])
        for bi in range(ntok // 128):
            c0 = bi * 128
            for c in range(DC):
                bank = ps[c % 8]
                tr(bank[:, 0:128], xT[:, c, c0:c0 + 128], IDF, [xT, kf], [bank])
                cp(stage[:, c * 128:(c + 1) * 128], bank[:, 0:128], [bank], [bufB], eng=("act" if c % 2 else "dve"))
            out_dmas.append(P.dma(Q_O, tile["y"][c0:c0 + 128, :], stage[:, :D], R=[bufB], W=[("ydram", id(tile), bi)]))

    tiles = []
    nt = SEQ // TT
    for i in range(nt):
        run = {"zero": i == 0, "blocks": list(range(TT // 128)), "key": "pstate", "okey": "pstate",
               "C": pC, "n": pn, "m": pm, "S": pS, "Co": pC, "no": pn, "mo": pm, "So": pS}
        tiles.append({"ntok": TT, "x": x_p[i * TT:(i + 1) * TT, :], "y": y_p[i * TT:(i + 1) * TT, :],
                      "segs": [(0, TT, 0)], "blocks": [(j * 128, 128) for j in range(TT // 128)], "runs": [run]})
    sruns = []
    for s in range(NSAMP):
        sruns.append({"zero": False, "blocks": [s], "key": "sstate_in", "okey": "sstate_out",
                      "C": sC_in[:, s], "n": sn_in[:, s], "m": sm_in[:, s], "S": sS_in[:, s],
                      "Co": sC[:, s], "no": sn[:, s], "mo": sm[:, s], "So": sS[:, s]})
    tiles.append({"ntok": NSAMP * SLEN, "x": x_s, "y": y_s, "segs": [(s * SLEN, SLEN, 1 + s) for s in range(NSAMP)],
                  "blocks": [(s * SLEN, SLEN) for s in range(NSAMP)], "runs": sruns})

    load_x(tiles[0])
    cast_weights(0)
    mod_phase()
    cast_weights(1)
    for ti, tile in enumerate(tiles):
        if ti > 0:
            load_x(tile)
        for l in range(DEPTH):
            adaln(l, 0, tile["ntok"], tile["segs"])
            mixer(l, tile)
            adaln(l, 1, tile["ntok"], tile["segs"])
            ffn(l, tile)
        store_y(tile)

    P.emit([("dma", d) for d in out_dmas])
    es.close()
    return nc


def _consts():
    kf = np.zeros((128, 4, 128), np.float32)
    kf[:, 0, :] = np.eye(128)
    s = np.arange(128)[:, None]
    t = np.arange(128)[None, :]
    kf[:, 1, :] = (s <= t)
    kf[:, 2, :] = np.where(t <= s, 0.0, NEG)
    kf[:, 3, :] = 1.0
    kb = np.zeros((128, 4, 128), np.float32)
    kb[:, 0, :] = np.eye(128)
    kb[:, 1, :] = 1.0
    kb[:, 2, :] = ((s // 16) == (t // 16)) & (s <= t)
    kb[:, 3, :8] = ((np.arange(128)[:, None] // 16) == np.arange(8)[None, :])
    ksc = np.ones((128, TT), np.float32)
    ksc[:, ::16] = 0.0
    return kf, kb.astype(ml_dtypes.bfloat16), ksc


_CACHE = {}


def kernel(x_prompt, x_sample, state_mlstm_C, state_mlstm_n, state_mlstm_m, state_hgrn_S,
           c_prompt, c_sample, w_mod, b_mod, norm1_g, w_in, b_gate, lower_bounds,
           mlstm_norm_g, hgrn_norm_g, w_out, norm2_g, w_up, w_down, final_g, _ncores=8):
    f = lambda a: np.ascontiguousarray(np.asarray(a), dtype=np.float32)
    x_prompt, x_sample = f(x_prompt), f(x_sample)
    B, SEQ = x_prompt.shape[0], x_prompt.shape[1]
    ncores = _ncores
    if SEQ not in _CACHE:
        _CACHE[SEQ] = build_program(SEQ)
    nc = _CACHE[SEQ]
    kf, kb, ksc = _consts()
    shared = {
        "w_mod": f(w_mod), "b_mod": f(b_mod).reshape(DEPTH * 192, 128), "norm1_g": f(norm1_g).reshape(DEPTH * DC, 128),
        "w_in": f(w_in), "b_gate": f(b_gate), "lower_bounds": f(lower_bounds).reshape(DEPTH * 16, 128),
        "mlstm_norm_g": f(mlstm_norm_g), "hgrn_norm_g": f(hgrn_norm_g).reshape(DEPTH * 16, 128),
        "w_out": f(w_out), "norm2_g": f(norm2_g).reshape(DEPTH * DC, 128), "w_up": f(w_up), "w_down": f(w_down),
        "final_g": f(final_g).reshape(DC, 128), "k_f32": kf, "k_bf": kb, "k_scan": ksc,
    }
    sC_, sn_, sm_, sS_ = f(state_mlstm_C), f(state_mlstm_n), f(state_mlstm_m), f(state_hgrn_S)
    c_prompt, c_sample = f(c_prompt), f(c_sample)
    in_maps = []
    for c in range(ncores):
        pb = c % B
        ss = [(NSAMP * c + i) % x_sample.shape[0] for i in range(NSAMP)]
        m = dict(shared)
        m["x_p"] = x_prompt[pb]
        m["x_s"] = np.ascontiguousarray(x_sample[ss].reshape(NSAMP * SLEN, D))
        m["c_all"] = np.ascontiguousarray(np.concatenate([c_prompt[pb:pb + 1], c_sample[ss]], axis=0).reshape(3 * DC, 128))
        m["sC_in"] = np.ascontiguousarray(sC_[:, ss])
        m["sn_in"] = np.ascontiguousarray(sn_[:, ss])
        m["sm_in"] = np.ascontiguousarray(sm_[:, ss])
        m["sS_in"] = np.ascontiguousarray(sS_[:, ss])
        in_maps.append(m)
    res = run_bass_kernel_spmd(nc, in_maps, core_ids=list(range(ncores)))
    R = res.results
    nS = x_sample.shape[0]
    y_prompt = np.stack([R[b]["y_p"] for b in range(B)]) if ncores >= B else np.stack([R[0]["y_p"]])
    def gather_s(name, axis_first):
        outs = [None] * nS
        for c in range(ncores):
            for i in range(NSAMP):
                sidx = NSAMP * c + i
                if sidx < nS and outs[sidx] is None:
                    a = R[c][name]
                    outs[sidx] = a[i * SLEN:(i + 1) * SLEN] if axis_first else a[:, i]
        return outs
    ys = gather_s("y_s", True)
    have = [o for o in ys if o is not None]
    y_sample = np.stack(have)
    nb = min(B, ncores)
    pCo = np.stack([R[b]["pC"] for b in range(nb)], axis=1)
    pno = np.stack([R[b]["pn"] for b in range(nb)], axis=1)
    pmo = np.stack([R[b]["pm"] for b in range(nb)], axis=1)
    pSo = np.stack([R[b]["pS"] for b in range(nb)], axis=1)
    def st(name):
        o = [x for x in gather_s(name, False) if x is not None]
        return np.stack(o, axis=1)
    return (y_prompt.astype(np.float32), y_sample.astype(np.float32), pCo, pno, pmo, pSo,
            st("sC"), st("sn"), st("sm"), st("sS"))
```
